# Optimizing a Trainium2 kernel written in Bass

```python
import jax, jax.numpy as jnp
from jax import lax
import numpy as np

D_MODEL = 1024
BATCH = 8
SEQ = 2048
DEPTH = 4
DEC_BATCH = 128
DEC_SEQ = 1
PAST_LEN = 16384
PAGE_SIZE = 128

N_MIXERS = 3
N_A = (DEPTH + 2) // 3
N_B = (DEPTH + 1) // 3
N_C = DEPTH // 3
CONV_A_W = 3
POOL_WINDOWS = (2, 4, 8, 16)
N_POOL_GROUPS = len(POOL_WINDOWS)
POOL_GROUP = D_MODEL // N_POOL_GROUPS
POOL_HIST = max(POOL_WINDOWS) - 1
CONV_C_W = 31
C_INNER = D_MODEL
MLP_HIDDEN = 4 * D_MODEL
EPS = 1e-6

kernel_name = "hybrid_conv_pool_conformer_decoder_step"


def rms_norm(x, g):
    xf = x.astype(jnp.float32)
    y = xf * lax.rsqrt(jnp.mean(xf * xf, axis=-1, keepdims=True) + EPS)
    return (y * g.astype(jnp.float32)).astype(x.dtype)


def layer_norm(x, g, b):
    xf = x.astype(jnp.float32)
    mu = jnp.mean(xf, axis=-1, keepdims=True)
    var = jnp.mean(jnp.square(xf - mu), axis=-1, keepdims=True)
    y = (xf - mu) * lax.rsqrt(var + EPS)
    return (y * g.astype(jnp.float32) + b.astype(jnp.float32)).astype(x.dtype)


def causal_dwconv(buf, w):
    c = buf.shape[-1]
    return lax.conv_general_dilated(
        buf, w[:, None, :].astype(buf.dtype), window_strides=(1,), padding='VALID',
        dimension_numbers=('NWC', 'WIO', 'NWC'), feature_group_count=c)


def short_conv_mixer(xn, state, w_in, conv_w, w_out):
    bch = jnp.einsum('bld,de->ble', xn, w_in)
    b, c, h = jnp.split(bch, 3, axis=-1)
    buf = jnp.concatenate([state, c * h], axis=1)
    z = causal_dwconv(buf, conv_w)
    y = jnp.einsum('bld,de->ble', b * z, w_out)
    return y, buf[:, -(CONV_A_W - 1):]


def pool_mixer(xn, state, w_group, scale, pos0):
    L = xn.shape[1]
    buf = jnp.concatenate([state, xn], axis=1)
    cs = jnp.cumsum(buf.astype(jnp.float32), axis=1)
    cs = jnp.pad(cs, ((0, 0), (1, 0), (0, 0)))
    end = cs[:, POOL_HIST + 1:]
    pos = (jnp.arange(L) + pos0).astype(jnp.float32)
    means = []
    for gi, w in enumerate(POOL_WINDOWS):
        sl = slice(gi * POOL_GROUP, (gi + 1) * POOL_GROUP)
        start = cs[:, POOL_HIST + 1 - w: POOL_HIST + 1 - w + L, sl]
        count = jnp.minimum(jnp.float32(w), pos + 1.0)[None, :, None]
        means.append((end[..., sl] - start) / count)
    mean = jnp.concatenate(means, axis=-1)
    d = (mean - xn.astype(jnp.float32)).astype(xn.dtype)
    d = d.reshape(d.shape[0], L, N_POOL_GROUPS, POOL_GROUP)
    z = jnp.einsum('blgc,gce->blge', d, w_group).reshape(d.shape[0], L, D_MODEL)
    return z * scale, buf[:, -POOL_HIST:]


def conformer_conv_mixer(xn, state, w_pw1, b_pw1, conv_w, conv_b, ln_g, ln_b, w_pw2, b_pw2):
    ag = jnp.einsum('bld,de->ble', xn, w_pw1) + b_pw1
    a, g = jnp.split(ag, 2, axis=-1)
    buf = jnp.concatenate([state, a * jax.nn.sigmoid(g)], axis=1)
    z = causal_dwconv(buf, conv_w) + conv_b
    z = layer_norm(z, ln_g, ln_b)
    y = jnp.einsum('ble,ed->bld', jax.nn.silu(z), w_pw2) + b_pw2
    return y, buf[:, -(CONV_C_W - 1):]


def sq_relu_mlp(xn, w1, w2):
    h = jax.nn.relu(jnp.einsum('bld,dh->blh', xn, w1))
    return jnp.einsum('blh,hd->bld', h * h, w2)


def trunk(x, st_a, st_b, st_c, pos0,
          norm_mix_g, norm_mlp_g, norm_final_g,
          a_w_in, a_conv_w, a_w_out,
          b_w_group, b_scale,
          c_w_pw1, c_b_pw1, c_conv_w, c_conv_b, c_ln_g, c_ln_b, c_w_pw2, c_b_pw2,
          mlp_w1, mlp_w2):
    new_a, new_b, new_c = [], [], []
    ia = ib = ic = 0
    for i in range(DEPTH):
        xn = rms_norm(x, norm_mix_g[i])
        kind = i % N_MIXERS
        if kind == 0:
            y, s = short_conv_mixer(xn, st_a[ia], a_w_in[ia], a_conv_w[ia], a_w_out[ia])
            new_a.append(s); ia += 1
        elif kind == 1:
            y, s = pool_mixer(xn, st_b[ib], b_w_group[ib], b_scale[ib], pos0)
            new_b.append(s); ib += 1
        else:
            y, s = conformer_conv_mixer(xn, st_c[ic], c_w_pw1[ic], c_b_pw1[ic], c_conv_w[ic], c_conv_b[ic],
                                        c_ln_g[ic], c_ln_b[ic], c_w_pw2[ic], c_b_pw2[ic])
            new_c.append(s); ic += 1
        x = x + y
        x = x + sq_relu_mlp(rms_norm(x, norm_mlp_g[i]), mlp_w1[i], mlp_w2[i])
    return rms_norm(x, norm_final_g), jnp.stack(new_a), jnp.stack(new_b), jnp.stack(new_c)


def setup_inputs(seed: int = 0) -> dict:
    key = jax.random.key(seed)
    ks = jax.random.split(key, 32)
    f32 = jnp.float32

    def nrm(k, shape, scale=1.0):
        return jax.random.normal(k, shape, f32) * scale

    D = D_MODEL
    return {
        "x_prompt": nrm(ks[0], (BATCH, SEQ, D)),
        "x_sample": nrm(ks[1], (DEC_BATCH, DEC_SEQ, D)),
        "state_conv_a": nrm(ks[2], (N_A, DEC_BATCH, CONV_A_W - 1, D)),
        "state_pool": nrm(ks[3], (N_B, DEC_BATCH, POOL_HIST, D)),
        "state_conv_c": nrm(ks[4], (N_C, DEC_BATCH, CONV_C_W - 1, C_INNER), 0.5),
        "norm_mix_g": 1.0 + nrm(ks[5], (DEPTH, D), 0.02),
        "norm_mlp_g": 1.0 + nrm(ks[6], (DEPTH, D), 0.02),
        "norm_final_g": 1.0 + nrm(ks[7], (D,), 0.02),
        "a_w_in": nrm(ks[8], (N_A, D, 3 * D), D ** -0.5),
        "a_conv_w": nrm(ks[9], (N_A, CONV_A_W, D), CONV_A_W ** -0.5),
        "a_w_out": nrm(ks[10], (N_A, D, D), D ** -0.5),
        "b_w_group": nrm(ks[11], (N_B, N_POOL_GROUPS, POOL_GROUP, POOL_GROUP), POOL_GROUP ** -0.5),
        "b_scale": 1.0 + nrm(ks[12], (N_B, D), 0.1),
        "c_w_pw1": nrm(ks[13], (N_C, D, 2 * C_INNER), D ** -0.5),
        "c_b_pw1": nrm(ks[14], (N_C, 2 * C_INNER), 0.02),
        "c_conv_w": nrm(ks[15], (N_C, CONV_C_W, C_INNER), CONV_C_W ** -0.5),
        "c_conv_b": nrm(ks[16], (N_C, C_INNER), 0.02),
        "c_ln_g": 1.0 + nrm(ks[17], (N_C, C_INNER), 0.02),
        "c_ln_b": nrm(ks[18], (N_C, C_INNER), 0.02),
        "c_w_pw2": nrm(ks[19], (N_C, C_INNER, D), C_INNER ** -0.5),
        "c_b_pw2": nrm(ks[20], (N_C, D), 0.02),
        "mlp_w1": nrm(ks[21], (DEPTH, D, MLP_HIDDEN), D ** -0.5),
        "mlp_w2": nrm(ks[22], (DEPTH, MLP_HIDDEN, D), MLP_HIDDEN ** -0.5),
    }


def reference(x_prompt, x_sample, state_conv_a, state_pool, state_conv_c,
              norm_mix_g, norm_mlp_g, norm_final_g,
              a_w_in, a_conv_w, a_w_out,
              b_w_group, b_scale,
              c_w_pw1, c_b_pw1, c_conv_w, c_conv_b, c_ln_g, c_ln_b, c_w_pw2, c_b_pw2,
              mlp_w1, mlp_w2):
    B = x_prompt.shape[0]
    dt = x_prompt.dtype
    zero_a = jnp.zeros((N_A, B, CONV_A_W - 1, D_MODEL), dt)
    zero_b = jnp.zeros((N_B, B, POOL_HIST, D_MODEL), dt)
    zero_c = jnp.zeros((N_C, B, CONV_C_W - 1, C_INNER), dt)
    y_prompt, na_p, nb_p, nc_p = trunk(
        x_prompt, zero_a, zero_b, zero_c, 0,
        norm_mix_g, norm_mlp_g, norm_final_g, a_w_in, a_conv_w, a_w_out, b_w_group, b_scale,
        c_w_pw1, c_b_pw1, c_conv_w, c_conv_b, c_ln_g, c_ln_b, c_w_pw2, c_b_pw2, mlp_w1, mlp_w2)
    y_sample, na_s, nb_s, nc_s = trunk(
        x_sample, state_conv_a, state_pool, state_conv_c, PAST_LEN,
        norm_mix_g, norm_mlp_g, norm_final_g, a_w_in, a_conv_w, a_w_out, b_w_group, b_scale,
        c_w_pw1, c_b_pw1, c_conv_w, c_conv_b, c_ln_g, c_ln_b, c_w_pw2, c_b_pw2, mlp_w1, mlp_w2)
    return (y_prompt, y_sample, na_p, na_s, nb_p, nb_s, nc_p, nc_s)
```

```python
import numpy as np
import concourse.bass as bass
import concourse.mybir as mybir
from concourse.bass_utils import run_bass_kernel_spmd

F32 = mybir.dt.float32
BF16 = mybir.dt.bfloat16
ALU = mybir.AluOpType
AF = mybir.ActivationFunctionType
AX = mybir.AxisListType

D = 1024
NCH = 8
TP = 2048
TS = 16
T = TP + TS
TILES = [(0, 512), (512, 512), (1024, 512), (1536, 512), (2048, 16)]
EPS = 1e-6
NSLAB = 4
SLAB = 2048
POOLW = (2, 4, 8, 16)

A1_OFF, A1_LEN = 0, 16512
A3_OFF, A3_LEN = 16512, 16520
A2_OFF, A2_LEN = 33032, 16752
AR_LEN = A2_OFF + A2_LEN

R_GMIX, R_GMLP, R_GFIN, R_ACONV, R_BSCALE, R_BPW1A, R_BPW1G, R_CCONV, R_CONVB, R_LNG, R_LNB, R_BPW2 = \
    0, 4, 8, 9, 15, 16, 17, 18, 49, 50, 51, 52
NPR = 53

ENGS = ("pe", "act", "dve", "pool", "sp")


class Plan:
    def __init__(self):
        self.q = {e: [] for e in ENGS}
        self.cnt = {e: 0 for e in ENGS}
        self.seen = {e: {} for e in ENGS}
        self.dmacnt = {}

    def _waits(self, eng, waits):
        wl = []
        for w in waits:
            if w is None:
                continue
            k, v = w
            if self.seen[eng].get(k, 0) >= v:
                continue
            self.seen[eng][k] = v
            wl.append((k, v))
        return wl

    def op(self, eng, fn, waits=(), inc=True):
        wl = self._waits(eng, waits)
        tok = None
        if inc:
            self.cnt[eng] += 1
            tok = (eng, self.cnt[eng])
        self.q[eng].append((fn, wl, inc, None))
        return tok

    def dma(self, eng, fn, sem, waits=()):
        wl = self._waits(eng, waits)
        self.dmacnt[sem] = self.dmacnt.get(sem, 0) + 16
        self.q[eng].append((fn, wl, False, sem))
        return (sem, self.dmacnt[sem])

    def wait_only(self, eng, waits):
        wl = self._waits(eng, waits)
        if wl:
            self.q[eng].append((None, wl, False, None))


def build_nc(depth_run=4):
    nc = bass.Bass("TRN2", target_bir_lowering=False)

    def din(name, shape):
        return nc.dram_tensor(name, shape, F32, kind="ExternalInput").ap()

    def dout(name, shape):
        return nc.dram_tensor(name, shape, F32, kind="ExternalOutput").ap()

    xp = din("xp", [TP, D])
    xs = din("xs", [TS, D])
    sa = din("sa", [2, TS, 2, D])
    spl = din("spl", [TS, 15, D])
    sc = din("sc", [TS, 30, D])
    norm_mix_g = din("norm_mix_g", [4, D])
    norm_mlp_g = din("norm_mlp_g", [4, D])
    norm_final_g = din("norm_final_g", [1, D])
    a_w_in = din("a_w_in", [2, D, 3 * D])
    a_conv_w = din("a_conv_w", [6, D])
    a_w_out = din("a_w_out", [2, D, D])
    b_w_group = din("b_w_group", [4, 256, 256])
    b_scale = din("b_scale", [1, D])
    c_w_pw1 = din("c_w_pw1", [D, 2 * D])
    c_b_pw1 = din("c_b_pw1", [2, D])
    c_conv_w = din("c_conv_w", [31, D])
    c_conv_b = din("c_conv_b", [1, D])
    c_ln_g = din("c_ln_g", [1, D])
    c_ln_b = din("c_ln_b", [1, D])
    c_w_pw2 = din("c_w_pw2", [D, D])
    c_b_pw2 = din("c_b_pw2", [1, D])
    mlp_w1 = din("mlp_w1", [4, D, 4 * D])
    mlp_w2 = din("mlp_w2", [4, 4 * D, D])

    yp = dout("yp", [TP, D])
    ys = dout("ys", [TS, D])
    nap = dout("nap", [2, 2, D])
    nas = dout("nas", [2, TS, 2, D])
    nbp = dout("nbp", [15, D])
    nbs = dout("nbs", [TS, 15, D])
    ncp = dout("ncp", [30, D])
    ncs = dout("ncs", [TS, 30, D])

    P = Plan()
    sem_names = list(ENGS[:4]) + ["w%d" % i for i in range(NSLAB)] + \
        ["prm", "xt0", "xt1", "xt2", "xt3", "st0", "st1", "o0", "o1", "o2", "o3", "dd", "xss"]

    import contextlib
    es = contextlib.ExitStack()
    with es:
        Xt = es.enter_context(nc.sbuf_tensor("X", [128, NCH, T], F32))
        ARt = es.enter_context(nc.sbuf_tensor("AR", [128, AR_LEN], BF16))
        WRt = es.enter_context(nc.sbuf_tensor("WR", [128, NSLAB * SLAB], BF16))
        SSt = es.enter_context(nc.sbuf_tensor("SS", [128, 4, 1024], F32))
        PRt = es.enter_context(nc.sbuf_tensor("PARAM", [128, NCH, NPR], F32))
        IDENTt = es.enter_context(nc.sbuf_tensor("IDENT", [128, 128], F32))
        IDBt = es.enter_context(nc.sbuf_tensor("IDB", [128, 128], BF16))
        ONESt = es.enter_context(nc.sbuf_tensor("ONES", [128, 128], BF16))
        DGt = es.enter_context(nc.sbuf_tensor("DG", [128, 8, 128], BF16))
        EPSt = es.enter_context(nc.sbuf_tensor("EPSB", [128, 1], F32))
        CNTt = es.enter_context(nc.sbuf_tensor("CNT", [128, 16], F32))
        HAt = es.enter_context(nc.sbuf_tensor("HA", [128, 2, NCH, TS], F32))
        HPt = es.enter_context(nc.sbuf_tensor("HP", [128, NCH, TS], F32))
        HCt = es.enter_context(nc.sbuf_tensor("HC", [128, NCH, TS], F32))
        TAt = es.enter_context(nc.sbuf_tensor("TAILA", [128, 2, NCH, 2], F32))
        CHSt = es.enter_context(nc.sbuf_tensor("CHS", [128, 2, NCH, TS], F32))
        TBt = es.enter_context(nc.sbuf_tensor("TAILB", [128, NCH, 15], F32))
        XNSt = es.enter_context(nc.sbuf_tensor("XNS", [128, NCH, TS], F32))
        TCt = es.enter_context(nc.sbuf_tensor("TAILC", [128, NCH, 30], F32))
        USt = es.enter_context(nc.sbuf_tensor("US", [128, NCH, TS], F32))
        TMPt = es.enter_context(nc.sbuf_tensor("TMPS", [128, 128], F32))
        IDTt = TMPt
        PSt = es.enter_context(nc.psum_tensor("PS", [128, 8, 512], F32))
        sems = {n: es.enter_context(nc.semaphore("s_" + n)) for n in sem_names}

        X = Xt.ap()
        AR = ARt.ap()
        WR = WRt.ap()
        SS = SSt.ap()
        PARAM = PRt.ap()
        IDENT = IDENTt.ap()
        IDB = IDBt.ap()
        ONES = ONESt.ap()
        DG = DGt.ap()
        PS = PSt.ap()

        def prm(c, r):
            return PARAM[:, c, r:r + 1]

        def arb(off, n):
            return AR[:, off:off + n]

        def arf(off, nf):
            return AR[:, off:off + 2 * nf].bitcast(F32)

        XN = arb(A1_OFF, NCH * T).rearrange("p (c t) -> p c t", c=NCH)
        SQ3 = arb(A3_OFF, NCH * T).rearrange("p (c t) -> p c t", c=NCH)
        SQ1 = XN
        A2B = arb(A2_OFF, NCH * T).rearrange("p (c t) -> p c t", c=NCH)
        H0 = A2B
        H1 = SQ3
        CH = arf(A3_OFF, 2 * 2066).rearrange("p (a t) -> p a t", a=2)
        ZA = arf(A3_OFF + 2 * 2 * 2066, 2 * T).rearrange("p (a t) -> p a t", a=2)
        SIG = arf(A3_OFF, 2 * T).rearrange("p (a t) -> p a t", a=2)
        U = arb(A2_OFF, NCH * 2094).rearrange("p (c t) -> p c t", c=NCH)
        ZC = arf(A1_OFF, NCH * T).rearrange("p (c t) -> p c t", c=NCH)
        XNF = arf(A3_OFF, NCH * T).rearrange("p (c t) -> p c t", c=NCH)
        YF = XNF
        ZBQ = arb(A2_OFF, 2 * 2 * NCH * 512).rearrange("p (b q c t) -> p b q c t", b=2, q=2, c=NCH)
        XT = arf(A1_OFF, 4 * 4 * 1024).rearrange("p (b i d) -> p b i d", b=4, i=4)
        OST = arf(A1_OFF, 4 * 1024).rearrange("p (b d) -> p b d", b=4)
        STG = arf(A2_OFF, 2 * 1024).rearrange("p (b d) -> p b d", b=2)
        PT = arf(A2_OFF + 4096, 1024)
        XSS = arf(A2_OFF + 6144, 1024)
        CTMP = TMPt.ap()

        class Banks:
            def __init__(self):
                self.n = 0
                self.rel = [[] for _ in range(8)]

            def alloc(self):
                for _ in range(8):
                    b = self.n % 8
                    self.n += 1
                    if self.rel[b] is not None:
                        break
                w = self.rel[b]
                assert w is not None, "all PSUM banks held: bank reused before release registered"
                self.rel[b] = None
                return b, list(w)

            def release(self, b, toks):
                self.rel[b] = [t for t in toks if t is not None]

        BK = Banks()

        class Ring:
            def __init__(self):
                self.n = 0
                self.free = [[] for _ in range(NSLAB)]
                self.hn = 0
                self.hlast = {}

            def gate(self, toks):
                self.free = [list(toks) for _ in range(NSLAB)]

            def load(self, fn_views, eng="pool"):
                s = self.n % NSLAB
                self.n += 1
                assert self.free[s] is not None, "weight-ring slot reused before its consumer was registered"
                slot = WR[:, s * SLAB:(s + 1) * SLAB]
                toks = []
                fr = self.free[s]
                for (of, ia) in fn_views:
                    o = of(slot)
                    if eng == "pool":
                        sname = "w%d" % s
                    else:
                        sname = "st%d" % (self.hn % 2)
                        self.hn += 1
                    toks.append(P.dma(eng, (lambda e, o=o, ia=ia: e.dma_start(out=o, in_=ia)),
                                      sname, waits=fr + ([self.hlast[sname]] if sname in self.hlast else [])))
                    if eng != "pool":
                        self.hlast[sname] = toks[-1]
                self.free[s] = None
                return s, slot, [toks[-1]]

            def done(self, s, tok):
                self.free[s] = [tok]

        RG = Ring()

        def slab_std(w_ap, r0, c0):
            src = w_ap[r0:r0 + 1024, c0:c0 + 256].rearrange("(k p) n -> p k n", p=128)
            return [((lambda slot: slot.rearrange("p (k n) -> p k n", k=8)), src)]

        def mm(out, lhsT, rhs, start, stop, waits, inc):
            return P.op("pe", (lambda e, out=out, lhsT=lhsT, rhs=rhs, start=start, stop=stop:
                               e.matmul(out, lhsT, rhs, start=start, stop=stop)), waits, inc=inc)

        def tp(out, in_, ident, waits, inc=True):
            return P.op("pe", (lambda e, out=out, in_=in_, ident=ident: e.transpose(out, in_, ident)),
                        waits, inc=inc)

        def act(out, in_, func, waits, bias=None, scale=None):
            kw = {}
            if bias is not None:
                kw["bias"] = bias
            if scale is not None:
                kw["scale"] = scale
            return P.op("act", (lambda e, out=out, in_=in_, func=func, kw=kw:
                                e.activation(out=out, in_=in_, func=func, **kw)), waits)

        def dve_tt(out, in0, in1, op, waits):
            return P.op("dve", (lambda e, out=out, in0=in0, in1=in1, op=op:
                                e.tensor_tensor(out=out, in0=in0, in1=in1, op=op)), waits)

        def dve_stt(out, in0, scalar, in1, op0, op1, waits):
            return P.op("dve", (lambda e, out=out, in0=in0, scalar=scalar, in1=in1, op0=op0, op1=op1:
                                e.scalar_tensor_tensor(out=out, in0=in0, scalar=scalar, in1=in1,
                                                       op0=op0, op1=op1)), waits)

        def dve_ts(out, in0, s1, s2, op0, op1, waits):
            if s2 is None:
                return P.op("dve", (lambda e, out=out, in0=in0, s1=s1, op0=op0:
                                    e.tensor_scalar(out=out, in0=in0, scalar1=s1, scalar2=None, op0=op0)),
                            waits)
            return P.op("dve", (lambda e, out=out, in0=in0, s1=s1, s2=s2, op0=op0, op1=op1:
                                e.tensor_scalar(out=out, in0=in0, scalar1=s1, scalar2=s2, op0=op0, op1=op1)),
                        waits)

        def dve_copy(out, in_, waits):
            return P.op("dve", (lambda e, out=out, in_=in_: e.tensor_copy(out=out, in_=in_)), waits)

        def dve_recip(out, in_, waits):
            return P.op("dve", (lambda e, out=out, in_=in_: e.reciprocal(out=out, in_=in_)), waits)

        def dve_memset(ap, val, waits=()):
            return P.op("dve", (lambda e, ap=ap, val=val: e.memset(ap, val)), waits)

        def dve_reduce(out, in_, waits):
            return P.op("dve", (lambda e, out=out, in_=in_: e.tensor_reduce(out=out, in_=in_, axis=AX.X,
                                                                             op=ALU.add)), waits)

        def sp_dma(out, in_, sem, waits=()):
            return P.dma("sp", (lambda e, out=out, in_=in_: e.dma_start(out=out, in_=in_)), sem, waits)

        out_toks = []

        ctl_ = [None]

        def ctl():
            return [ctl_[0]] if ctl_[0] is not None else []

        def snap():
            return [(e, P.cnt[e]) for e in ("pe", "act", "dve") if P.cnt[e] > 0]

        t_iota = P.op("pool", lambda e: e.iota(IDTt.ap(), [[1, 128]], base=0, channel_multiplier=-1,
                                               allow_small_or_imprecise_dtypes=True))
        t_cnt = P.op("pool", lambda e: e.iota(CNTt.ap(), [[1, 16]], base=1, channel_multiplier=0,
                                              allow_small_or_imprecise_dtypes=True))
        t_id = dve_ts(IDENT, IDTt.ap(), 0.0, None, ALU.is_equal, None, [t_iota])
        t_idb = dve_copy(IDB, IDENT, [t_id])
        t_ones = dve_memset(ONES, 1.0 / D)
        t_eps = dve_memset(EPSt.ap(), EPS)
        setup_tok = [t_id, t_idb, t_ones, t_eps, t_cnt]

        prm_loads = [(R_GMIX, 4, norm_mix_g), (R_GMLP, 4, norm_mlp_g), (R_GFIN, 1, norm_final_g),
                     (R_ACONV, 6, a_conv_w), (R_BSCALE, 1, b_scale), (R_BPW1A, 2, c_b_pw1),
                     (R_CCONV, 31, c_conv_w), (R_CONVB, 1, c_conv_b), (R_LNG, 1, c_ln_g),
                     (R_LNB, 1, c_ln_b), (R_BPW2, 1, c_b_pw2)]
        ptok = None
        for (r0, n, src) in prm_loads:
            ptok = sp_dma(PT[r0:r0 + n, :], src[:, :], "prm")

        def xload(tt, waits):
            return sp_dma(XT[:, tt], xp[tt * 512:(tt + 1) * 512, :].rearrange("(i p) d -> p i d", p=128),
                          "xt%d" % tt, waits=waits)
        xlds = {tt: xload(tt, []) for tt in range(4)}
        RG.gate([xlds[3]])
        b, bw = BK.alloc()
        pv = PS[:, b, :].rearrange("p (c r) -> p c r", c=NCH)
        tk = None
        for c in range(NCH):
            tk = tp(pv[:, c, 0:NPR], PT[0:NPR, c * 128:(c + 1) * 128], IDENT[0:NPR, 0:NPR],
                    [ptok] + setup_tok + bw)
        t_param = dve_copy(PARAM, pv[:, :, 0:NPR], [tk])
        BK.release(b, [t_param])

        x_tok = [[] for _ in TILES]
        x_rd = [[] for _ in TILES]
        flip = 0
        for tt in range(4):
            bf = tt
            ld = xlds[tt]
            for c in range(NCH):
                b, bw = BK.alloc()
                for i in range(4):
                    tk = tp(PS[:, b, i * 128:(i + 1) * 128], XT[:, bf, i, c * 128:(c + 1) * 128], IDENT,
                            [ld, t_id] + bw, inc=(i == 3))
                if flip == 0:
                    ev = act(X[:, c, tt * 512:(tt + 1) * 512], PS[:, b, :], AF.Copy, [tk])
                else:
                    ev = dve_copy(X[:, c, tt * 512:(tt + 1) * 512], PS[:, b, :], [tk])
                flip ^= 1
                BK.release(b, [ev])
                x_tok[tt].append(ev)
        ld = sp_dma(XSS[0:TS, :], xs[:, :], "xss")
        b, bw = BK.alloc()
        for c in range(NCH):
            tk = tp(PS[:, b, c * TS:(c + 1) * TS], XSS[0:TS, c * 128:(c + 1) * 128], IDENT[0:TS, 0:TS],
                    [ld, t_id] + bw, inc=(c == NCH - 1))
        ev = dve_copy(X[:, :, TP:T], PS[:, b, 0:NCH * TS].rearrange("p (c t) -> p c t", c=NCH), [tk])
        BK.release(b, [ev])
        x_tok[4].append(ev)

        import collections as _coll
        bg = _coll.deque()
        pre = {"a0": [], "a1": [], "p": [], "c": []}

        def bg_step(n=1):
            while bg and n > 0:
                bg.popleft()()
                n -= 1

        def stage_rows(src_ap, nrows):
            views = [((lambda slot: slot.bitcast(F32)[0:nrows, :]), src_ap)]
            s_, slot, rdy = RG.load(views, eng="sp")
            return s_, slot.bitcast(F32), rdy

        def unit_a(ia):
            st = {}

            def s0():
                s_, sv, rdy = stage_rows(sa[ia].rearrange("b r d -> (b r) d"), 32)
                b, bw = BK.alloc()
                tk = None
                for c in range(NCH):
                    tk = tp(PS[:, b, c * 32:(c + 1) * 32], sv[0:32, c * 128:(c + 1) * 128], IDENT[0:32, 0:32],
                            rdy + [t_id] + bw, inc=(c == NCH - 1))
                RG.done(s_, tk)
                st["b"], st["tk"], st["ev"] = b, tk, None
            steps = [s0]
            for c in range(NCH):
                def sc_(c=c):
                    v = PS[:, st["b"], 0:256].rearrange("p (c b r) -> p c b r", c=NCH, r=2)
                    e1 = dve_ts(HAt.ap()[:, ia, c, :], v[:, c, :, 0], prm(c, R_ACONV + 3 * ia + 0), None,
                                ALU.mult, None, [st["tk"], t_param])
                    st["ev"] = dve_stt(HAt.ap()[:, ia, c, :], v[:, c, :, 1], prm(c, R_ACONV + 3 * ia + 1),
                                       HAt.ap()[:, ia, c, :], ALU.mult, ALU.add, [e1])
                    if c == NCH - 1:
                        BK.release(st["b"], [st["ev"]])
                        pre["a%d" % ia] = [st["ev"]]
                steps.append(sc_)
            return steps

        def unit_rows(kind, i):
            st = {}
            if kind == "p":
                src = spl[8 * i:8 * i + 8].rearrange("b r d -> (b r) d")
            else:
                src = sc[4 * i:4 * i + 4].rearrange("b r d -> (b r) d")

            def s_load():
                st["s"], st["sv"], st["rdy"] = stage_rows(src, 120)
            steps = [s_load]
            for half in range(2):
                def s_tp(half=half):
                    b, bw = BK.alloc()
                    tk = None
                    for cc in range(4):
                        c = half * 4 + cc
                        tk = tp(PS[:, b, cc * 120:(cc + 1) * 120], st["sv"][0:120, c * 128:(c + 1) * 128],
                                IDENT[0:120, 0:120], st["rdy"] + [t_id] + bw, inc=(cc == 3))
                    if half == 1:
                        RG.done(st["s"], tk)
                    st["b"], st["tk"], st["ev"] = b, tk, None
                steps.append(s_tp)
                for cc in range(4):
                    def s_red(half=half, cc=cc):
                        c = half * 4 + cc
                        b, tk = st["b"], st["tk"]
                        if kind == "p":
                            w = POOLW[c // 2]
                            v = PS[:, b, cc * 120:(cc + 1) * 120].rearrange("p (b r) -> p b r", r=15)
                            st["ev"] = dve_reduce(HPt.ap()[:, c, 8 * i:8 * i + 8], v[:, :, 15 - (w - 1):15], [tk])
                        else:
                            v = PS[:, b, cc * 120:(cc + 1) * 120].rearrange("p (b r) -> p b r", r=30)
                            wv = PARAM[:, c, R_CCONV:R_CONVB - 1].unsqueeze(1).broadcast_to([128, 4, 30])
                            tmpv = CTMP[:, 0:120].rearrange("p (b r) -> p b r", r=30)
                            e1 = dve_tt(tmpv, v, wv, ALU.mult, [tk, t_param] + ctl())
                            st["ev"] = dve_reduce(HCt.ap()[:, c, 4 * i:4 * i + 4], tmpv, [e1])
                            ctl_[0] = st["ev"]
                        if cc == 3:
                            BK.release(b, [st["ev"]])
                            pre[kind] = [st["ev"]]
                    steps.append(s_red)
            return steps

        bg_a0 = unit_a(0)
        for u in [unit_rows("p", 0), unit_rows("p", 1)] + [unit_rows("c", i) for i in range(4)] + [unit_a(1)]:
            bg.extend(u)

        for ia in range(2):
            out_toks.append(sp_dma(nas[ia, :, 0, :], sa[ia, :, 1, :], "dd"))
        out_toks.append(sp_dma(nbs[:, 0:14, :], spl[:, 1:15, :], "dd"))
        out_toks.append(sp_dma(ncs[:, 0:29, :], sc[:, 1:30, :], "dd"))

        def rmsnorm(grow, dst, sqbuf, war, extra=None, lookahead=True):
            out_tok = [None] * len(TILES)
            st = {}
            last_pe = [None]

            def s1(ti):
                t0, n = TILES[ti]
                a1 = act(sqbuf[:, 0:4, t0:t0 + n], X[:, 0:4, t0:t0 + n], AF.Square, x_tok[ti] + war)
                a1b = act(sqbuf[:, 4:8, t0:t0 + n], X[:, 4:8, t0:t0 + n], AF.Square, x_tok[ti] + war)
                b, bw = BK.alloc()
                tk = None
                for c in range(NCH):
                    tk = mm(PS[:, b, 0:n], ONES, sqbuf[:, c, t0:t0 + n], c == 0, c == NCH - 1,
                            [a1 if c < 4 else a1b, t_ones] + bw, inc=(c == NCH - 1))
                st[ti] = (b, tk, a1, a1b)
                last_pe[0] = tk

            def s2(ti):
                t0, n = TILES[ti]
                b, tk, a1, a1b = st[ti]
                a2 = act(PS[:, b, 0:n], PS[:, b, 0:n], AF.Ln, [tk, t_eps], bias=EPSt.ap(), scale=1.0)
                a3 = act(PS[:, b, 0:n], PS[:, b, 0:n], AF.Exp, [a2], scale=-0.5)
                toks = []
                dl = None
                for c in range(NCH):
                    dl = dve_stt(dst[:, c, t0:t0 + n], X[:, c, t0:t0 + n], prm(c, grow), PS[:, b, 0:n],
                                 ALU.mult, ALU.mult, [a3, t_param] + x_tok[ti] + war)
                    toks.append(dl)
                if extra is not None:
                    ex = extra(ti, PS[:, b, 0:n], a3)
                    if ex is not None:
                        dl = ex
                BK.release(b, [dl])
                x_rd[ti] = [a1, a1b]
                out_tok[ti] = toks

            nt = len(TILES)
            if lookahead:
                s1(0)
                for ti in range(nt):
                    if ti + 1 < nt:
                        s1(ti + 1)
                    s2(ti)
            else:
                for ti in range(nt):
                    s1(ti)
                    s2(ti)
            return out_tok, last_pe[0]

        def per_k(toklists):
            return lambda ti, k: [toklists[ti][k]]

        def per_tile(toklists):
            return lambda ti, k: toklists[ti]

        def proj(slabs, nk, rhs_fn, rhs_tok, evac, wslice=None, first_to=False, last_to=False, all_to=False):
            last = None
            if all_to:
                assert len(slabs) <= NSLAB
                ld = [RG.load(views) for (views, chunks) in slabs]
                tk = None
                for ti in range(len(TILES)):
                    t0, n = TILES[ti]
                    for si, (views, chunks) in enumerate(slabs):
                        s, slot, rdy = ld[si]
                        for j, ch in enumerate(chunks):
                            b, bw = BK.alloc()
                            for k in range(nk):
                                tk = mm(PS[:, b, 0:n], wslice(slot, j, k), rhs_fn(ch, k, t0, n), k == 0,
                                        k == nk - 1, rdy + bw + rhs_tok(ti, k), inc=(k == nk - 1))
                            BK.release(b, evac(ch, ti, t0, n, PS[:, b, 0:n], tk))
                for (s, slot, rdy) in ld:
                    RG.done(s, tk)
                return tk
            for si, (views, chunks) in enumerate(slabs):
                s, slot, rdy = RG.load(views)
                to = (first_to and si == 0) or (last_to and si == len(slabs) - 1)
                if to:
                    order = [(j, ti) for ti in range(len(TILES)) for j in range(len(chunks))]
                else:
                    order = [(j, ti) for j in range(len(chunks)) for ti in range(len(TILES))]
                tk = None
                for (j, ti) in order:
                    ch = chunks[j]
                    t0, n = TILES[ti]
                    b, bw = BK.alloc()
                    for k in range(nk):
                        lhsT = wslice(slot, j, k)
                        tk = mm(PS[:, b, 0:n], lhsT, rhs_fn(ch, k, t0, n), k == 0, k == nk - 1,
                                rdy + bw + rhs_tok(ti, k), inc=(k == nk - 1))
                    rel = evac(ch, ti, t0, n, PS[:, b, 0:n], tk)
                    BK.release(b, rel)
                RG.done(s, tk)
                last = tk
            return last

        def wstd(slot, j, k):
            return slot.rearrange("p (k n) -> p k n", k=8)[:, k, j * 128:(j + 1) * 128]

        def resid_evac(bias_row=None):
            newtok = [[] for _ in TILES]

            def ev(ch, ti, t0, n, bank, tk):
                w = [tk] + x_rd[ti]
                if bias_row is None:
                    e = dve_tt(X[:, ch, t0:t0 + n], bank, X[:, ch, t0:t0 + n], ALU.add, w)
                else:
                    e = dve_stt(X[:, ch, t0:t0 + n], bank, prm(ch, bias_row), X[:, ch, t0:t0 + n],
                                ALU.add, ALU.add, w + [t_param])
                newtok[ti] = [e]
                return [e]
            return ev, newtok

        def mlp(li, xn_tok, a1_war_tok, hook=None):
            w1 = mlp_w1[li]
            w2 = mlp_w2[li]
            Hb = [H0, H1]
            h_war = [[], []]
            h_tok = [None, None]
            rt_n = [0]
            rt_free = [[] for _ in range(4)]
            rnew = [[[] for _ in TILES]]

            def do_w1(g):
                hb = Hb[g % 2]
                toks = [[] for _ in TILES]

                def ev(ch, ti, t0, n, bank, tk):
                    r = rt_n[0] % 4
                    rt_n[0] += 1
                    rt = SS[:, r, 0:n]
                    a = act(rt, bank, AF.Relu, [tk] + rt_free[r])
                    d = dve_tt(hb[:, ch, t0:t0 + n], rt, bank, ALU.mult, [a] + h_war[g % 2])
                    rt_free[r] = [d]
                    toks[ti] = [d]
                    bg_step(1)
                    return [d]
                slabs = [(slab_std(w1, 0, g * 1024 + s * 256), [2 * s, 2 * s + 1]) for s in range(4)]
                proj(slabs, 8, lambda ch, k, t0, n: XN[:, k, t0:t0 + n], per_k(xn_tok), ev, wslice=wstd,
                     first_to=(g == 0))
                h_tok[g % 2] = toks

            def do_w2(g):
                hb = Hb[g % 2]
                ev, newtok = resid_evac()
                slabs = [(slab_std(w2, g * 1024, s * 256), [2 * s, 2 * s + 1]) for s in range(4)]
                last = proj(slabs, 8, lambda ch, k, t0, n: hb[:, k, t0:t0 + n], per_tile(h_tok[g % 2]), ev,
                            wslice=wstd, all_to=(g == 3))
                h_war[g % 2] = [last]
                for ti in range(len(TILES)):
                    x_tok[ti] = newtok[ti]
                    x_rd[ti] = []

            h_war[0] = list(a1_war_tok)
            h_war[1] = list(a1_war_tok)
            do_w1(0)
            for g in range(1, 4):
                do_w1(g)
                if g == 3:
                    bg_step(10 ** 9)
                if g == 3 and hook is not None:
                    hook([P.cnt["pe"] and ("pe", P.cnt["pe"])])
                do_w2(g - 1)
            do_w2(3)

        def mixer_a(ia, xn_tok, last_stat_pe):
            win = a_w_in[ia]
            wout = a_w_out[ia]
            ch_war = [[last_stat_pe], [last_stat_pe]]
            z_war = [[last_stat_pe], [last_stat_pe]]
            bz_tok = [[] for _ in TILES]
            hs_tok = {}
            z_tok = {}
            pad = dve_memset(CH[:, :, 0:2], 0.0, [last_stat_pe])

            import collections
            pend = collections.deque()
            key_last = {}
            seq = [0]

            def flush_n(n):
                while pend and n > 0:
                    pend.popleft()[1]()
                    n -= 1

            def flush_until(key):
                lim = key_last[key]
                while pend and pend[0][0] <= lim:
                    pend.popleft()[1]()

            def push_conv(a, ti, ch, e):
                w0, w1, w2 = (prm(ch, R_ACONV + 3 * ia + j) for j in range(3))
                st = {}
                zw = list(z_war[a])
                ops = []
                if ti < 4:
                    t0, n = TILES[ti]

                    def f1():
                        st["z"] = dve_ts(ZA[:, a, t0:t0 + n], CH[:, a, t0:t0 + n], w0, None, ALU.mult, None,
                                         [e, pad, t_param] + zw)

                    def f2():
                        st["z"] = dve_stt(ZA[:, a, t0:t0 + n], CH[:, a, t0 + 1:t0 + 1 + n], w1,
                                          ZA[:, a, t0:t0 + n], ALU.mult, ALU.add, [st["z"]])

                    def f3():
                        st["z"] = dve_stt(ZA[:, a, t0:t0 + n], CH[:, a, t0 + 2:t0 + 2 + n], w2,
                                          ZA[:, a, t0:t0 + n], ALU.mult, ALU.add, [st["z"]])
                        z_tok[(a, ti)] = st["z"]
                    ops = [f1, f2, f3]
                    if ti == 3:
                        def f4():
                            st["z"] = dve_copy(TAt.ap()[:, ia, ch, :], CH[:, a, TP:TP + 2], [st["z"]])
                            tail_toks.append(st["z"])
                        ops.append(f4)
                else:
                    def g1():
                        st["z"] = dve_stt(ZA[:, a, TP:T], CH[:, a, 2 + TP:2 + T], w2, HAt.ap()[:, ia, ch, :],
                                          ALU.mult, ALU.add, [e, t_param] + pre["a%d" % ia] + zw)
                        z_tok[(a, ti)] = st["z"]

                    def g2():
                        st["z"] = dve_copy(CHSt.ap()[:, ia, ch, :], CH[:, a, 2 + TP:2 + T], [st["z"]])
                        ch_war[a] = [st["z"]]
                        tail_toks.append(st["z"])
                    ops = [g1, g2]
                for f in ops:
                    seq[0] += 1
                    pend.append((seq[0], f))
                key_last[(a, ti)] = seq[0]

            for q in range(4):
                def ev_h(ch, ti, t0, n, bank, tk):
                    a = ch % 2
                    e = act(CH[:, a, 2 + t0:2 + t0 + n], bank, AF.Copy, [tk] + ch_war[a])
                    hs_tok[(a, ti)] = e
                    return [e]

                def ev_c(ch, ti, t0, n, bank, tk):
                    a = ch % 2
                    e = dve_tt(CH[:, a, 2 + t0:2 + t0 + n], bank, CH[:, a, 2 + t0:2 + t0 + n], ALU.mult,
                               [tk, hs_tok[(a, ti)]])
                    push_conv(a, ti, ch, e)
                    flush_n(2)
                    return [e]

                def ev_b(ch, ti, t0, n, bank, tk):
                    a = ch % 2
                    flush_until((a, ti))
                    e = dve_tt(A2B[:, ch, t0:t0 + n], bank, ZA[:, a, t0:t0 + n], ALU.mult,
                               [tk, z_tok[(a, ti)]])
                    bz_tok[ti] = [e]
                    if ti == 4:
                        z_war[a] = [e]
                    return [e]

                chs = [2 * q, 2 * q + 1]
                rf = (lambda ch, k, t0, n: XN[:, k, t0:t0 + n])
                proj([(slab_std(win, 0, 2048 + 256 * q), chs)], 8, rf, per_k(xn_tok), ev_h, wslice=wstd,
                     first_to=(q == 0))
                proj([(slab_std(win, 0, 1024 + 256 * q), chs)], 8, rf, per_k(xn_tok), ev_c, wslice=wstd)
                proj([(slab_std(win, 0, 256 * q), chs)], 8, rf, per_k(xn_tok), ev_b, wslice=wstd)
                assert not pend

            war = snap()
            ev, newtok = resid_evac()
            slabs = [(slab_std(wout, 0, s * 256), [2 * s, 2 * s + 1]) for s in range(4)]
            last = proj(slabs, 8, lambda ch, k, t0, n: A2B[:, k, t0:t0 + n], per_tile(bz_tok), ev, wslice=wstd,
                        all_to=True)
            for ti in range(len(TILES)):
                x_tok[ti] = newtok[ti]
                x_rd[ti] = []
            return war

        def mixer_b(xn_tok, last_stat_pe, grow):
            Dm = A2B
            coefs = [1.0 / w for w in POOLW] + [1.0 / w - 1.0 for w in POOLW]
            dgt = None
            for r, cf in enumerate(coefs):
                dgt = act(DG[:, r, :], IDB, AF.Copy, [t_idb, last_stat_pe], scale=float(cf))
            d_tok = [[] for _ in TILES]
            allx = [t for l in xn_tok for t in l]
            for c in range(NCH):
                gi = c // 2
                w = POOLW[gi]
                for ti in range(4):
                    t0 = ti * 512
                    b, bw = BK.alloc()
                    tk = None
                    for j in range(w):
                        lhsT = DG[:, (4 + gi) if j == 0 else gi, :]
                        if ti == 0:
                            o = PS[:, b, j:512]
                            r_ = XN[:, c, 0:512 - j]
                        else:
                            o = PS[:, b, :]
                            r_ = XN[:, c, t0 - j:t0 - j + 512]
                        tk = mm(o, lhsT, r_, j == 0, j == w - 1, [dgt] + bw + allx, inc=(j == w - 1))
                    e = act(Dm[:, c, t0:t0 + 512], PS[:, b, :], AF.Copy, [tk, last_stat_pe])
                    rel = [e]
                    if ti == 0 and w > 1:
                        hw = w - 1
                        f1 = dve_tt(CTMP[:, 0:hw], PS[:, b, 0:hw], XN[:, c, 0:hw], ALU.add, [tk, e] + ctl())
                        f2 = dve_tt(CTMP[:, 0:hw], CTMP[:, 0:hw], RWt.ap()[:, gi, 0:hw], ALU.mult, [f1, t_rw])
                        f3 = dve_tt(Dm[:, c, 0:hw], CTMP[:, 0:hw], XN[:, c, 0:hw], ALU.subtract, [f2, e])
                        ctl_[0] = f3
                        rel = [e, f1]
                        e = f3
                    BK.release(b, rel)
                    d_tok[ti] = [e]
                e5 = dve_tt(CTMP[:, 64:64 + TS], HPt.ap()[:, c, :], XNSt.ap()[:, c, :], ALU.add,
                            allx + pre["p"] + [last_stat_pe] + ctl())
                e6 = dve_stt(Dm[:, c, TP:T], CTMP[:, 64:64 + TS], 1.0 / w, XNSt.ap()[:, c, :],
                             ALU.mult, ALU.subtract, [e5])
                ctl_[0] = e6
                d_tok[4] = [e6]
            alld = [t for l in d_tok for t in l]
            war = snap()
            newtok = [[] for _ in TILES]

            def evb(ch, ti, t0, n, bank, tk):
                e = dve_stt(X[:, ch, t0:t0 + n], bank, prm(ch, R_BSCALE), X[:, ch, t0:t0 + n],
                            ALU.mult, ALU.add, [tk, t_param] + x_rd[ti])
                newtok[ti] = [e]
                return [e]
            src = b_w_group.rearrange("g (k p) n -> p g k n", p=128)
            views = [((lambda slot: slot.rearrange("p (g k n) -> p g k n", g=4, k=2)), src)]
            s, slot, rdy = RG.load(views)
            sv = slot.rearrange("p (g k n) -> p g k n", g=4, k=2)
            tk = None
            for ti, (t0, n) in enumerate(TILES):
                for ch in range(NCH):
                    gi, a = ch // 2, ch % 2
                    b, bw = BK.alloc()
                    for k in range(2):
                        tk = mm(PS[:, b, 0:n], sv[:, gi, k, a * 128:(a + 1) * 128], Dm[:, 2 * gi + k, t0:t0 + n],
                                k == 0, k == 1, rdy + bw + alld, inc=(k == 1))
                    BK.release(b, evb(ch, ti, t0, n, PS[:, b, 0:n], tk))
            RG.done(s, tk)
            for ti in range(len(TILES)):
                x_tok[ti] = newtok[ti]
                x_rd[ti] = []
            return war

        def mixer_c(xn_tok, last_stat_pe):
            sg_tok = {}
            u_tok = [[] for _ in TILES]
            sig_war = [[last_stat_pe], [last_stat_pe]]
            padu = dve_memset(U[:, :, 0:30], 0.0, [last_stat_pe])
            lastu = [None]
            for q in range(4):
                def ev_g(ch, ti, t0, n, bank, tk):
                    a = ch % 2
                    e = act(SIG[:, a, t0:t0 + n], bank, AF.Sigmoid, [tk, t_param] + sig_war[a],
                            bias=prm(ch, R_BPW1G), scale=1.0)
                    sg_tok[(a, ti)] = e
                    return [e]

                def ev_a(ch, ti, t0, n, bank, tk):
                    a = ch % 2
                    w = [tk, sg_tok[(a, ti)], t_param, padu]
                    if ti < 4:
                        e = dve_stt(U[:, ch, 30 + t0:30 + t0 + n], bank, prm(ch, R_BPW1A), SIG[:, a, t0:t0 + n],
                                    ALU.add, ALU.mult, w)
                        if ti == 3:
                            e = dve_stt(TCt.ap()[:, ch, :], bank[:, 512 - 30:512], prm(ch, R_BPW1A),
                                        SIG[:, a, TP - 30:TP], ALU.add, ALU.mult, [e])
                    else:
                        e = dve_stt(USt.ap()[:, ch, :], bank, prm(ch, R_BPW1A), SIG[:, a, TP:T],
                                    ALU.add, ALU.mult, w)
                        sig_war[a] = [e]
                    u_tok[ti] = [e]
                    lastu[0] = e
                    return [e]
                chs = [2 * q, 2 * q + 1]
                rf = (lambda ch, k, t0, n: XN[:, k, t0:t0 + n])
                proj([(slab_std(c_w_pw1, 0, 1024 + 256 * q), chs)], 8, rf, per_k(xn_tok), ev_g, wslice=wstd,
                     first_to=(q == 0))
                last_pw1 = proj([(slab_std(c_w_pw1, 0, 256 * q), chs)], 8, rf, per_k(xn_tok), ev_a, wslice=wstd)
            dg_n = 0
            dg_free = [[last_stat_pe] for _ in range(8)]
            z_tok = [[] for _ in TILES]
            zlast = None
            for c in range(NCH):
                bks = [BK.alloc() for _ in range(4)]
                tk = None
                for j in range(31):
                    r = dg_n % 8
                    dg_n += 1
                    dtk = act(DG[:, r, :], IDB, AF.Copy, [t_idb, t_param] + dg_free[r],
                              scale=prm(c, R_CCONV + j))
                    for ti in range(4):
                        t0 = ti * 512
                        b, bw = bks[ti]
                        tk = mm(PS[:, b, :], DG[:, r, :], U[:, c, t0 + j:t0 + j + 512], j == 0, j == 30,
                                [dtk, lastu[0], last_pw1] + (bw if j == 0 else []), inc=(j == 30 or ti == 3))
                    dg_free[r] = [tk]
                for ti in range(4):
                    t0 = ti * 512
                    b, _ = bks[ti]
                    e = dve_ts(ZC[:, c, t0:t0 + 512], PS[:, b, :], prm(c, R_CONVB), None, ALU.add, None,
                               [tk, last_pw1, lastu[0], t_param])
                    BK.release(b, [e])
                    zlast = e
                e1 = dve_stt(ZC[:, c, TP:T], USt.ap()[:, c, :], prm(c, R_CCONV + 30), HCt.ap()[:, c, :],
                             ALU.mult, ALU.add, [last_pw1, lastu[0], t_param] + pre["c"])
                e2 = dve_ts(ZC[:, c, TP:T], ZC[:, c, TP:T], prm(c, R_CONVB), None, ALU.add, None, [e1])
                zs_last = e2
            conv_pe = tk
            SL = A2B
            ZBQ = SSt.ap().rearrange("p a t -> p (a t)").bitcast(BF16).rearrange(
                "p (b q c t) -> p b q c t", b=2, q=2, c=NCH)
            HT = [(t0, 256) for t0 in range(0, TP, 256)] + [(TP, TS)]
            sl_tok = [[] for _ in TILES]
            zq_free = [[], []]
            hb = {}

            def stage_a(h):
                t0, n = HT[h]
                bf = h % 2
                a1 = act(ZBQ[:, bf, 0, :, 0:n], ZC[:, :, t0:t0 + n], AF.Copy, [zlast, zs_last] + zq_free[bf])
                a2 = act(ZBQ[:, bf, 1, :, 0:n], ZC[:, :, t0:t0 + n], AF.Square, [a1])
                b, bw = BK.alloc()
                tk = None
                for c in range(NCH):
                    tk = mm(PS[:, b, 0:n], ONES, ZBQ[:, bf, 0, c, 0:n], c == 0, c == NCH - 1, [a1, a2] + bw,
                            inc=False)
                for c in range(NCH):
                    tk = mm(PS[:, b, 256:256 + n], ONES, ZBQ[:, bf, 1, c, 0:n], c == 0, c == NCH - 1, [a2],
                            inc=(c == NCH - 1))
                zq_free[bf] = [tk]
                hb[h] = (b, tk)

            def stage_le(h):
                t0, n = HT[h]
                b, tk = hb[h]
                mu = PS[:, b, 0:n]
                vr = PS[:, b, 256:256 + n]
                ms = MUS[:, (h % 2) * 256:(h % 2) * 256 + n]
                c0 = dve_copy(ms, mu, [tk])
                c1 = dve_tt(ms, ms, mu, ALU.mult, [c0])
                c2 = dve_tt(vr, vr, ms, ALU.subtract, [c1])
                a3 = act(vr, vr, AF.Ln, [c2, t_eps], bias=EPSt.ap(), scale=1.0)
                a4 = act(vr, vr, AF.Exp, [a3], scale=-0.5)
                hb[h] = (b, a4)

            def stage_ap(h):
                t0, n = HT[h]
                b, a4 = hb[h]
                mu = PS[:, b, 0:n]
                vr = PS[:, b, 256:256 + n]
                dl = None
                d3 = None
                for c in range(NCH):
                    d2 = dve_tt(ZC[:, c, t0:t0 + n], ZC[:, c, t0:t0 + n], mu, ALU.subtract, [a4])
                    d3 = dve_tt(ZC[:, c, t0:t0 + n], ZC[:, c, t0:t0 + n], vr, ALU.mult, [d2])
                    dl = act(SL[:, c, t0:t0 + n], ZC[:, c, t0:t0 + n], AF.Silu, [d3, conv_pe, t_param],
                             bias=prm(c, R_LNB), scale=prm(c, R_LNG))
                BK.release(b, [d3])
                sl_tok[min(t0 // 512, 4)] = [dl]

            pairs = [(0, 1), (2, 3), (4, 5), (6, 7), (8,)]
            for h in pairs[0]:
                stage_a(h)
            for pi, pr in enumerate(pairs):
                for h in pr:
                    stage_le(h)
                if pi + 1 < len(pairs):
                    for h in pairs[pi + 1]:
                        stage_a(h)
                for h in pr:
                    stage_ap(h)
            war = snap()
            ev, newtok = resid_evac(R_BPW2)
            slabs = [(slab_std(c_w_pw2, 0, s * 256), [2 * s, 2 * s + 1]) for s in range(4)]
            last = proj(slabs, 8, lambda ch, k, t0, n: SL[:, k, t0:t0 + n], per_tile(sl_tok), ev, wslice=wstd,
                        all_to=True)
            for ti in range(len(TILES)):
                x_tok[ti] = newtok[ti]
                x_rd[ti] = []
            return war

        RCNt = es.enter_context(nc.sbuf_tensor("RCN", [128, 16], F32))
        RCN = RCNt.ap()
        t_rcn = dve_recip(RCN, CNTt.ap(), [t_cnt])
        RWt = es.enter_context(nc.sbuf_tensor("RW", [128, 4, 16], F32))
        t_rw = None
        for gi_, w_ in enumerate(POOLW):
            t_rw = dve_ts(RWt.ap()[:, gi_, :], RCN, float(w_), None, ALU.mult, None, [t_rcn])
        MUSt = es.enter_context(nc.sbuf_tensor("MUS", [128, 512], F32))
        MUS = MUSt.ap()

        NOST = 4
        ost_free = [[] for _ in range(NOST)]
        ost_n = [0]
        ost_flip = [0]
        tail_toks = []
        small_done = [False]

        def emit_rows(src_fn, nrow, dst_ap, src_tok):
            bf = ost_n[0] % NOST
            ost_n[0] += 1
            evs = []
            for half in range(2):
                b, bw = BK.alloc()
                tk = None
                for cc in range(4):
                    c = half * 4 + cc
                    tk = tp(PS[0:nrow, b, cc * 128:(cc + 1) * 128], src_fn(c), IDENT, [t_id] + bw + src_tok,
                            inc=(cc == 3))
                o = OST[0:nrow, bf, half * 512:(half + 1) * 512]
                if ost_flip[0] == 0:
                    e = act(o, PS[0:nrow, b, :], AF.Copy, [tk] + ost_free[bf])
                else:
                    e = dve_copy(o, PS[0:nrow, b, :], [tk] + ost_free[bf])
                ost_flip[0] ^= 1
                BK.release(b, [e])
                evs.append(e)
            d = sp_dma(dst_ap, OST[0:nrow, bf, :], "o%d" % bf, waits=evs)
            ost_free[bf] = [d]
            out_toks.append(d)

        def emit_small(src_tok):
            small_done[0] = True
            na_done = min(2, (depth_run + 2) // 3)
            for ia_ in range(na_done):
                emit_rows(lambda c, ia_=ia_: TAt.ap()[:, ia_, c, :], 2, nap[ia_], src_tok)
                emit_rows(lambda c, ia_=ia_: CHSt.ap()[:, ia_, c, :], TS, nas[ia_, :, 1, :], src_tok)
            if depth_run >= 2:
                emit_rows(lambda c: TBt.ap()[:, c, :], 15, nbp[:, :], src_tok)
                emit_rows(lambda c: XNSt.ap()[:, c, :], TS, nbs[:, 14, :], src_tok)
            if depth_run >= 3:
                emit_rows(lambda c: TCt.ap()[:, c, :], 30, ncp[:, :], src_tok)
                emit_rows(lambda c: USt.ap()[:, c, :], TS, ncs[:, 29, :], src_tok)

        def small_hook(pe_tok):
            for bf in range(NOST):
                ost_free[bf] = list(pe_tok)
            emit_small(list(pe_tok) + tail_toks + [("dve", P.cnt["dve"])])

        a_war = []
        for tt in range(5):
            a_war += x_tok[tt]
        prev_last = a_war
        ia = 0
        for li in range(depth_run):
            kind = li % 3
            if kind == 1:
                def extra_b(ti, bank, rdy, li=li):
                    e = None
                    if ti == 3:
                        for c in range(NCH):
                            e = dve_stt(TBt.ap()[:, c, :], X[:, c, TP - 15:TP], prm(c, R_GMIX + li),
                                        bank[:, 512 - 15:512], ALU.mult, ALU.mult, [rdy, t_param])
                    elif ti == 4:
                        for c in range(NCH):
                            e = dve_stt(XNSt.ap()[:, c, :], X[:, c, TP:T], prm(c, R_GMIX + li),
                                        bank[:, 0:TS], ALU.mult, ALU.mult, [rdy, t_param])
                    return e
                xn_tok, stat_pe = rmsnorm(R_GMIX + li, XN, SQ3, prev_last, extra=extra_b)
                lastm = mixer_b(xn_tok, stat_pe, R_GMIX + li)
            else:
                xn_tok, stat_pe = rmsnorm(R_GMIX + li, XN, SQ3, prev_last)
                if li == 0:
                    for f in bg_a0:
                        f()
                if kind == 0:
                    lastm = mixer_a(ia, xn_tok, stat_pe)
                    ia += 1
                else:
                    lastm = mixer_c(xn_tok, stat_pe)
            xn_tok, stat_pe = rmsnorm(R_GMLP + li, XN, SQ3, lastm)
            mlp(li, xn_tok, [stat_pe], hook=(small_hook if li == depth_run - 1 else None))
            prev_last = []

        YF = X
        y_tok, stat_pe = rmsnorm(R_GFIN, YF, SQ1, prev_last + [t for l in ost_free for t in l])
        for bf in range(NOST):
            ost_free[bf] = ost_free[bf] + [stat_pe]
        for i in range(16):
            emit_rows(lambda c, i=i: YF[:, c, i * 128:(i + 1) * 128], 128, yp[i * 128:(i + 1) * 128, :],
                      y_tok[i // 4])
        emit_rows(lambda c: YF[:, c, TP:T], TS, ys[:, :], y_tok[4])
        if not small_done[0]:
            emit_small([t for l in y_tok for t in l])

        P.wait_only("sp", out_toks)

        with nc.Block() as block:
            def runner(name):
                def f(e):
                    for (fn, wl, inc, dsem) in P.q[name]:
                        for (k, v) in wl:
                            e.wait_ge(sems[k], v)
                        if fn is None:
                            continue
                        ins = fn(e)
                        if inc:
                            ins.then_inc(sems[name], 1)
                        elif dsem is not None:
                            ins.then_inc(sems[dsem], 16)
                return f
            block.tensor(runner("pe"))
            block.scalar(runner("act"))
            block.vector(runner("dve"))
            block.gpsimd(runner("pool"))
            block.sync(runner("sp"))
    return nc


_IN_KEYS = ["norm_mix_g", "norm_mlp_g", "a_w_in", "a_w_out", "mlp_w1", "mlp_w2"]


def make_in_maps(inputs, cores):
    f = lambda a: np.ascontiguousarray(np.asarray(a, dtype=np.float32))
    shared = {
        "norm_mix_g": f(inputs["norm_mix_g"]),
        "norm_mlp_g": f(inputs["norm_mlp_g"]),
        "norm_final_g": f(inputs["norm_final_g"]).reshape(1, D),
        "a_w_in": f(inputs["a_w_in"]),
        "a_conv_w": f(inputs["a_conv_w"]).reshape(6, D),
        "a_w_out": f(inputs["a_w_out"]),
        "b_w_group": f(inputs["b_w_group"]).reshape(4, 256, 256),
        "b_scale": f(inputs["b_scale"]).reshape(1, D),
        "c_w_pw1": f(inputs["c_w_pw1"]).reshape(D, 2 * D),
        "c_b_pw1": f(inputs["c_b_pw1"]).reshape(2, D),
        "c_conv_w": f(inputs["c_conv_w"]).reshape(31, D),
        "c_conv_b": f(inputs["c_conv_b"]).reshape(1, D),
        "c_ln_g": f(inputs["c_ln_g"]).reshape(1, D),
        "c_ln_b": f(inputs["c_ln_b"]).reshape(1, D),
        "c_w_pw2": f(inputs["c_w_pw2"]).reshape(D, D),
        "c_b_pw2": f(inputs["c_b_pw2"]).reshape(1, D),
        "mlp_w1": f(inputs["mlp_w1"]),
        "mlp_w2": f(inputs["mlp_w2"]),
    }
    xp = f(inputs["x_prompt"])
    xs = f(inputs["x_sample"])
    sa = f(inputs["state_conv_a"])
    sp = f(inputs["state_pool"])
    sc = f(inputs["state_conv_c"])
    maps = []
    for c in cores:
        m = dict(shared)
        m["xp"] = np.ascontiguousarray(xp[c])
        m["xs"] = np.ascontiguousarray(xs[16 * c:16 * c + 16, 0, :])
        m["sa"] = np.ascontiguousarray(sa[:, 16 * c:16 * c + 16])
        m["spl"] = np.ascontiguousarray(sp[0, 16 * c:16 * c + 16])
        m["sc"] = np.ascontiguousarray(sc[0, 16 * c:16 * c + 16])
        maps.append(m)
    return maps


def kernel(**inputs):
    n = 8
    nc = build_nc(4)
    maps = make_in_maps(inputs, list(range(n)))
    res = run_bass_kernel_spmd(nc, maps, core_ids=list(range(n)))
    r = res.results
    y_prompt = np.stack([r[c]["yp"] for c in range(n)], axis=0)
    y_sample = np.concatenate([r[c]["ys"] for c in range(n)], axis=0)[:, None, :]
    na_p = np.stack([r[c]["nap"] for c in range(n)], axis=1)
    na_s = np.concatenate([r[c]["nas"] for c in range(n)], axis=1)
    nb_p = np.stack([r[c]["nbp"] for c in range(n)], axis=0)[None]
    nb_s = np.concatenate([r[c]["nbs"] for c in range(n)], axis=0)[None]
    nc_p = np.stack([r[c]["ncp"] for c in range(n)], axis=0)[None]
    nc_s = np.concatenate([r[c]["ncs"] for c in range(n)], axis=0)[None]
    outs = (y_prompt, y_sample, na_p, na_s, nb_p, nb_s, nc_p, nc_s)
    return tuple(np.ascontiguousarray(o, dtype=np.float32) for o in outs)
```

```python
import numpy as np
import concourse.bass as bass
import concourse.mybir as mybir
from concourse.bass_utils import run_bass_kernel_spmd

F32 = mybir.dt.float32
BF16 = mybir.dt.bfloat16
ALU = mybir.AluOpType
AF = mybir.ActivationFunctionType
AX = mybir.AxisListType

D = 1024
NCH = 8
TP = 2048
TS = 16
T = TP + TS
TILES = [(0, 512), (512, 512), (1024, 512), (1536, 512), (2048, 16)]
EPS = 1e-6
NSLAB = 4
SLAB = 2048
POOLW = (2, 4, 8, 16)

A1_OFF, A1_LEN = 0, 16512
A3_OFF, A3_LEN = 16512, 16520
A2_OFF, A2_LEN = 33032, 16752
AR_LEN = A2_OFF + A2_LEN

R_GMIX, R_GMLP, R_GFIN, R_ACONV, R_BSCALE, R_BPW1A, R_BPW1G, R_CCONV, R_CONVB, R_LNG, R_LNB, R_BPW2 = \
    0, 4, 8, 9, 15, 16, 17, 18, 49, 50, 51, 52
NPR = 53

ENGS = ("pe", "act", "dve", "pool", "sp")


class Plan:
    def __init__(self):
        self.q = {e: [] for e in ENGS}
        self.cnt = {e: 0 for e in ENGS}
        self.seen = {e: {} for e in ENGS}
        self.dmacnt = {}

    def _waits(self, eng, waits):
        wl = []
        for w in waits:
            if w is None:
                continue
            k, v = w
            if self.seen[eng].get(k, 0) >= v:
                continue
            self.seen[eng][k] = v
            wl.append((k, v))
        return wl

    def op(self, eng, fn, waits=(), inc=True):
        wl = self._waits(eng, waits)
        tok = None
        if inc:
            self.cnt[eng] += 1
            tok = (eng, self.cnt[eng])
        self.q[eng].append((fn, wl, inc, None))
        return tok

    def dma(self, eng, fn, sem, waits=()):
        wl = self._waits(eng, waits)
        self.dmacnt[sem] = self.dmacnt.get(sem, 0) + 16
        self.q[eng].append((fn, wl, False, sem))
        return (sem, self.dmacnt[sem])

    def wait_only(self, eng, waits):
        wl = self._waits(eng, waits)
        if wl:
            self.q[eng].append((None, wl, False, None))


def build_nc(depth_run=4):
    nc = bass.Bass("TRN2", target_bir_lowering=False)

    def din(name, shape):
        return nc.dram_tensor(name, shape, F32, kind="ExternalInput").ap()

    def dout(name, shape):
        return nc.dram_tensor(name, shape, F32, kind="ExternalOutput").ap()

    xp = din("xp", [TP, D])
    xs = din("xs", [TS, D])
    sa = din("sa", [2, TS, 2, D])
    spl = din("spl", [TS, 15, D])
    sc = din("sc", [TS, 30, D])
    norm_mix_g = din("norm_mix_g", [4, D])
    norm_mlp_g = din("norm_mlp_g", [4, D])
    norm_final_g = din("norm_final_g", [1, D])
    a_w_in = din("a_w_in", [2, D, 3 * D])
    a_conv_w = din("a_conv_w", [6, D])
    a_w_out = din("a_w_out", [2, D, D])
    b_w_group = din("b_w_group", [4, 256, 256])
    b_scale = din("b_scale", [1, D])
    c_w_pw1 = din("c_w_pw1", [D, 2 * D])
    c_b_pw1 = din("c_b_pw1", [2, D])
    c_conv_w = din("c_conv_w", [31, D])
    c_conv_b = din("c_conv_b", [1, D])
    c_ln_g = din("c_ln_g", [1, D])
    c_ln_b = din("c_ln_b", [1, D])
    c_w_pw2 = din("c_w_pw2", [D, D])
    c_b_pw2 = din("c_b_pw2", [1, D])
    mlp_w1 = din("mlp_w1", [4, D, 4 * D])
    mlp_w2 = din("mlp_w2", [4, 4 * D, D])

    yp = dout("yp", [TP, D])
    ys = dout("ys", [TS, D])
    nap = dout("nap", [2, 2, D])
    nas = dout("nas", [2, TS, 2, D])
    nbp = dout("nbp", [15, D])
    nbs = dout("nbs", [TS, 15, D])
    ncp = dout("ncp", [30, D])
    ncs = dout("ncs", [TS, 30, D])

    P = Plan()
    sem_names = list(ENGS[:4]) + ["w%d" % i for i in range(NSLAB)] + \
        ["prm", "xt0", "xt1", "xt2", "xt3", "st0", "st1", "o0", "o1", "o2", "o3", "dd", "xss"]

    import contextlib
    es = contextlib.ExitStack()
    with es:
        Xt = es.enter_context(nc.sbuf_tensor("X", [128, NCH, T], F32))
        ARt = es.enter_context(nc.sbuf_tensor("AR", [128, AR_LEN], BF16))
        WRt = es.enter_context(nc.sbuf_tensor("WR", [128, NSLAB * SLAB], BF16))
        SSt = es.enter_context(nc.sbuf_tensor("SS", [128, 4, 1024], F32))
        PRt = es.enter_context(nc.sbuf_tensor("PARAM", [128, NCH, NPR], F32))
        IDENTt = es.enter_context(nc.sbuf_tensor("IDENT", [128, 128], F32))
        IDBt = es.enter_context(nc.sbuf_tensor("IDB", [128, 128], BF16))
        ONESt = es.enter_context(nc.sbuf_tensor("ONES", [128, 128], BF16))
        DGt = es.enter_context(nc.sbuf_tensor("DG", [128, 8, 128], BF16))
        EPSt = es.enter_context(nc.sbuf_tensor("EPSB", [128, 1], F32))
        CNTt = es.enter_context(nc.sbuf_tensor("CNT", [128, 16], F32))
        HAt = es.enter_context(nc.sbuf_tensor("HA", [128, 2, NCH, TS], F32))
        HPt = es.enter_context(nc.sbuf_tensor("HP", [128, NCH, TS], F32))
        HCt = es.enter_context(nc.sbuf_tensor("HC", [128, NCH, TS], F32))
        TAt = es.enter_context(nc.sbuf_tensor("TAILA", [128, 2, NCH, 2], F32))
        CHSt = es.enter_context(nc.sbuf_tensor("CHS", [128, 2, NCH, TS], F32))
        TBt = es.enter_context(nc.sbuf_tensor("TAILB", [128, NCH, 15], F32))
        XNSt = es.enter_context(nc.sbuf_tensor("XNS", [128, NCH, TS], F32))
        TCt = es.enter_context(nc.sbuf_tensor("TAILC", [128, NCH, 30], F32))
        USt = es.enter_context(nc.sbuf_tensor("US", [128, NCH, TS], F32))
        TMPt = es.enter_context(nc.sbuf_tensor("TMPS", [128, 128], F32))
        IDTt = TMPt
        PSt = es.enter_context(nc.psum_tensor("PS", [128, 8, 512], F32))
        sems = {n: es.enter_context(nc.semaphore("s_" + n)) for n in sem_names}

        X = Xt.ap()
        AR = ARt.ap()
        WR = WRt.ap()
        SS = SSt.ap()
        PARAM = PRt.ap()
        IDENT = IDENTt.ap()
        IDB = IDBt.ap()
        ONES = ONESt.ap()
        DG = DGt.ap()
        PS = PSt.ap()

        def prm(c, r):
            return PARAM[:, c, r:r + 1]

        def arb(off, n):
            return AR[:, off:off + n]

        def arf(off, nf):
            return AR[:, off:off + 2 * nf].bitcast(F32)

        XN = arb(A1_OFF, NCH * T).rearrange("p (c t) -> p c t", c=NCH)
        SQ3 = arb(A3_OFF, NCH * T).rearrange("p (c t) -> p c t", c=NCH)
        SQ1 = XN
        A2B = arb(A2_OFF, NCH * T).rearrange("p (c t) -> p c t", c=NCH)
        H0 = A2B
        H1 = SQ3
        CH = arf(A3_OFF, 2 * 2066).rearrange("p (a t) -> p a t", a=2)
        ZA = arf(A3_OFF + 2 * 2 * 2066, 2 * T).rearrange("p (a t) -> p a t", a=2)
        SIG = arf(A3_OFF, 2 * T).rearrange("p (a t) -> p a t", a=2)
        U = arb(A2_OFF, NCH * 2094).rearrange("p (c t) -> p c t", c=NCH)
        ZC = arf(A1_OFF, NCH * T).rearrange("p (c t) -> p c t", c=NCH)
        XNF = arf(A3_OFF, NCH * T).rearrange("p (c t) -> p c t", c=NCH)
        YF = XNF
        ZBQ = arb(A2_OFF, 2 * 2 * NCH * 512).rearrange("p (b q c t) -> p b q c t", b=2, q=2, c=NCH)
        XT = arf(A1_OFF, 4 * 4 * 1024).rearrange("p (b i d) -> p b i d", b=4, i=4)
        OST = arf(A1_OFF, 4 * 1024).rearrange("p (b d) -> p b d", b=4)
        STG = arf(A2_OFF, 2 * 1024).rearrange("p (b d) -> p b d", b=2)
        PT = arf(A2_OFF + 4096, 1024)
        XSS = arf(A2_OFF + 6144, 1024)
        CTMP = TMPt.ap()

        class Banks:
            def __init__(self):
                self.n = 0
                self.rel = [[] for _ in range(8)]

            def alloc(self):
                for _ in range(8):
                    b = self.n % 8
                    self.n += 1
                    if self.rel[b] is not None:
                        break
                w = self.rel[b]
                assert w is not None, "all PSUM banks held: bank reused before release registered"
                self.rel[b] = None
                return b, list(w)

            def release(self, b, toks):
                self.rel[b] = [t for t in toks if t is not None]

        BK = Banks()

        class Ring:
            def __init__(self):
                self.n = 0
                self.free = [[] for _ in range(NSLAB)]
                self.hn = 0
                self.hlast = {}

            def gate(self, toks):
                self.free = [list(toks) for _ in range(NSLAB)]

            def load(self, fn_views, eng="pool"):
                s = self.n % NSLAB
                self.n += 1
                assert self.free[s] is not None, "weight-ring slot reused before its consumer was registered"
                slot = WR[:, s * SLAB:(s + 1) * SLAB]
                toks = []
                fr = self.free[s]
                for (of, ia) in fn_views:
                    o = of(slot)
                    if eng == "pool":
                        sname = "w%d" % s
                    else:
                        sname = "st%d" % (self.hn % 2)
                        self.hn += 1
                    toks.append(P.dma(eng, (lambda e, o=o, ia=ia: e.dma_start(out=o, in_=ia)),
                                      sname, waits=fr + ([self.hlast[sname]] if sname in self.hlast else [])))
                    if eng != "pool":
                        self.hlast[sname] = toks[-1]
                self.free[s] = None
                return s, slot, [toks[-1]]

            def done(self, s, tok):
                self.free[s] = [tok]

        RG = Ring()

        def slab_std(w_ap, r0, c0):
            src = w_ap[r0:r0 + 1024, c0:c0 + 256].rearrange("(k p) n -> p k n", p=128)
            return [((lambda slot: slot.rearrange("p (k n) -> p k n", k=8)), src)]

        def mm(out, lhsT, rhs, start, stop, waits, inc):
            return P.op("pe", (lambda e, out=out, lhsT=lhsT, rhs=rhs, start=start, stop=stop:
                               e.matmul(out, lhsT, rhs, start=start, stop=stop)), waits, inc=inc)

        def tp(out, in_, ident, waits, inc=True):
            return P.op("pe", (lambda e, out=out, in_=in_, ident=ident: e.transpose(out, in_, ident)),
                        waits, inc=inc)

        def act(out, in_, func, waits, bias=None, scale=None):
            kw = {}
            if bias is not None:
                kw["bias"] = bias
            if scale is not None:
                kw["scale"] = scale
            return P.op("act", (lambda e, out=out, in_=in_, func=func, kw=kw:
                                e.activation(out=out, in_=in_, func=func, **kw)), waits)

        def dve_tt(out, in0, in1, op, waits):
            return P.op("dve", (lambda e, out=out, in0=in0, in1=in1, op=op:
                                e.tensor_tensor(out=out, in0=in0, in1=in1, op=op)), waits)

        def dve_stt(out, in0, scalar, in1, op0, op1, waits):
            return P.op("dve", (lambda e, out=out, in0=in0, scalar=scalar, in1=in1, op0=op0, op1=op1:
                                e.scalar_tensor_tensor(out=out, in0=in0, scalar=scalar, in1=in1,
                                                       op0=op0, op1=op1)), waits)

        def dve_ts(out, in0, s1, s2, op0, op1, waits):
            if s2 is None:
                return P.op("dve", (lambda e, out=out, in0=in0, s1=s1, op0=op0:
                                    e.tensor_scalar(out=out, in0=in0, scalar1=s1, scalar2=None, op0=op0)),
                            waits)
            return P.op("dve", (lambda e, out=out, in0=in0, s1=s1, s2=s2, op0=op0, op1=op1:
                                e.tensor_scalar(out=out, in0=in0, scalar1=s1, scalar2=s2, op0=op0, op1=op1)),
                        waits)

        def dve_copy(out, in_, waits):
            return P.op("dve", (lambda e, out=out, in_=in_: e.tensor_copy(out=out, in_=in_)), waits)

        def dve_recip(out, in_, waits):
            return P.op("dve", (lambda e, out=out, in_=in_: e.reciprocal(out=out, in_=in_)), waits)

        def dve_memset(ap, val, waits=()):
            return P.op("dve", (lambda e, ap=ap, val=val: e.memset(ap, val)), waits)

        def dve_reduce(out, in_, waits):
            return P.op("dve", (lambda e, out=out, in_=in_: e.tensor_reduce(out=out, in_=in_, axis=AX.X,
                                                                             op=ALU.add)), waits)

        def sp_dma(out, in_, sem, waits=()):
            return P.dma("sp", (lambda e, out=out, in_=in_: e.dma_start(out=out, in_=in_)), sem, waits)

        out_toks = []

        ctl_ = [None]

        def ctl():
            return [ctl_[0]] if ctl_[0] is not None else []

        def snap():
            return [(e, P.cnt[e]) for e in ("pe", "act", "dve") if P.cnt[e] > 0]

        t_iota = P.op("pool", lambda e: e.iota(IDTt.ap(), [[1, 128]], base=0, channel_multiplier=-1,
                                               allow_small_or_imprecise_dtypes=True))
        t_cnt = P.op("pool", lambda e: e.iota(CNTt.ap(), [[1, 16]], base=1, channel_multiplier=0,
                                              allow_small_or_imprecise_dtypes=True))
        t_id = dve_ts(IDENT, IDTt.ap(), 0.0, None, ALU.is_equal, None, [t_iota])
        t_idb = dve_copy(IDB, IDENT, [t_id])
        t_ones = dve_memset(ONES, 1.0 / D)
        t_eps = dve_memset(EPSt.ap(), EPS)
        setup_tok = [t_id, t_idb, t_ones, t_eps, t_cnt]

        prm_loads = [(R_GMIX, 4, norm_mix_g), (R_GMLP, 4, norm_mlp_g), (R_GFIN, 1, norm_final_g),
                     (R_ACONV, 6, a_conv_w), (R_BSCALE, 1, b_scale), (R_BPW1A, 2, c_b_pw1),
                     (R_CCONV, 31, c_conv_w), (R_CONVB, 1, c_conv_b), (R_LNG, 1, c_ln_g),
                     (R_LNB, 1, c_ln_b), (R_BPW2, 1, c_b_pw2)]
        ptok = None
        for (r0, n, src) in prm_loads:
            ptok = sp_dma(PT[r0:r0 + n, :], src[:, :], "prm")

        def xload(tt, waits):
            return sp_dma(XT[:, tt], xp[tt * 512:(tt + 1) * 512, :].rearrange("(i p) d -> p i d", p=128),
                          "xt%d" % tt, waits=waits)
        xlds = {tt: xload(tt, []) for tt in range(4)}
        RG.gate([xlds[3]])
        b, bw = BK.alloc()
        pv = PS[:, b, :].rearrange("p (c r) -> p c r", c=NCH)
        tk = None
        for c in range(NCH):
            tk = tp(pv[:, c, 0:NPR], PT[0:NPR, c * 128:(c + 1) * 128], IDENT[0:NPR, 0:NPR],
                    [ptok] + setup_tok + bw)
        t_param = dve_copy(PARAM, pv[:, :, 0:NPR], [tk])
        BK.release(b, [t_param])

        x_tok = [[] for _ in TILES]
        x_rd = [[] for _ in TILES]
        flip = 0
        for tt in range(4):
            bf = tt
            ld = xlds[tt]
            for c in range(NCH):
                b, bw = BK.alloc()
                for i in range(4):
                    tk = tp(PS[:, b, i * 128:(i + 1) * 128], XT[:, bf, i, c * 128:(c + 1) * 128], IDENT,
                            [ld, t_id] + bw, inc=(i == 3))
                if flip == 0:
                    ev = act(X[:, c, tt * 512:(tt + 1) * 512], PS[:, b, :], AF.Copy, [tk])
                else:
                    ev = dve_copy(X[:, c, tt * 512:(tt + 1) * 512], PS[:, b, :], [tk])
                flip ^= 1
                BK.release(b, [ev])
                x_tok[tt].append(ev)
        ld = sp_dma(XSS[0:TS, :], xs[:, :], "xss")
        b, bw = BK.alloc()
        for c in range(NCH):
            tk = tp(PS[:, b, c * TS:(c + 1) * TS], XSS[0:TS, c * 128:(c + 1) * 128], IDENT[0:TS, 0:TS],
                    [ld, t_id] + bw, inc=(c == NCH - 1))
        ev = dve_copy(X[:, :, TP:T], PS[:, b, 0:NCH * TS].rearrange("p (c t) -> p c t", c=NCH), [tk])
        BK.release(b, [ev])
        x_tok[4].append(ev)

        import collections as _coll
        bg = _coll.deque()
        pre = {"a0": [], "a1": [], "p": [], "c": []}

        def bg_step(n=1):
            while bg and n > 0:
                bg.popleft()()
                n -= 1

        def stage_rows(src_ap, nrows):
            views = [((lambda slot: slot.bitcast(F32)[0:nrows, :]), src_ap)]
            s_, slot, rdy = RG.load(views, eng="sp")
            return s_, slot.bitcast(F32), rdy

        def unit_a(ia):
            st = {}

            def s0():
                s_, sv, rdy = stage_rows(sa[ia].rearrange("b r d -> (b r) d"), 32)
                b, bw = BK.alloc()
                tk = None
                for c in range(NCH):
                    tk = tp(PS[:, b, c * 32:(c + 1) * 32], sv[0:32, c * 128:(c + 1) * 128], IDENT[0:32, 0:32],
                            rdy + [t_id] + bw, inc=(c == NCH - 1))
                RG.done(s_, tk)
                st["b"], st["tk"], st["ev"] = b, tk, None
            steps = [s0]
            for c in range(NCH):
                def sc_(c=c):
                    v = PS[:, st["b"], 0:256].rearrange("p (c b r) -> p c b r", c=NCH, r=2)
                    e1 = dve_ts(HAt.ap()[:, ia, c, :], v[:, c, :, 0], prm(c, R_ACONV + 3 * ia + 0), None,
                                ALU.mult, None, [st["tk"], t_param])
                    st["ev"] = dve_stt(HAt.ap()[:, ia, c, :], v[:, c, :, 1], prm(c, R_ACONV + 3 * ia + 1),
                                       HAt.ap()[:, ia, c, :], ALU.mult, ALU.add, [e1])
                    if c == NCH - 1:
                        BK.release(st["b"], [st["ev"]])
                        pre["a%d" % ia] = [st["ev"]]
                steps.append(sc_)
            return steps

        def unit_rows(kind, i):
            st = {}
            if kind == "p":
                src = spl[8 * i:8 * i + 8].rearrange("b r d -> (b r) d")
            else:
                src = sc[4 * i:4 * i + 4].rearrange("b r d -> (b r) d")

            def s_load():
                st["s"], st["sv"], st["rdy"] = stage_rows(src, 120)
            steps = [s_load]
            for half in range(2):
                def s_tp(half=half):
                    b, bw = BK.alloc()
                    tk = None
                    for cc in range(4):
                        c = half * 4 + cc
                        tk = tp(PS[:, b, cc * 120:(cc + 1) * 120], st["sv"][0:120, c * 128:(c + 1) * 128],
                                IDENT[0:120, 0:120], st["rdy"] + [t_id] + bw, inc=(cc == 3))
                    if half == 1:
                        RG.done(st["s"], tk)
                    st["b"], st["tk"], st["ev"] = b, tk, None
                steps.append(s_tp)
                for cc in range(4):
                    def s_red(half=half, cc=cc):
                        c = half * 4 + cc
                        b, tk = st["b"], st["tk"]
                        if kind == "p":
                            w = POOLW[c // 2]
                            v = PS[:, b, cc * 120:(cc + 1) * 120].rearrange("p (b r) -> p b r", r=15)
                            st["ev"] = dve_reduce(HPt.ap()[:, c, 8 * i:8 * i + 8], v[:, :, 15 - (w - 1):15], [tk])
                        else:
                            v = PS[:, b, cc * 120:(cc + 1) * 120].rearrange("p (b r) -> p b r", r=30)
                            wv = PARAM[:, c, R_CCONV:R_CONVB - 1].unsqueeze(1).broadcast_to([128, 4, 30])
                            tmpv = CTMP[:, 0:120].rearrange("p (b r) -> p b r", r=30)
                            e1 = dve_tt(tmpv, v, wv, ALU.mult, [tk, t_param] + ctl())
                            st["ev"] = dve_reduce(HCt.ap()[:, c, 4 * i:4 * i + 4], tmpv, [e1])
                            ctl_[0] = st["ev"]
                        if cc == 3:
                            BK.release(b, [st["ev"]])
                            pre[kind] = [st["ev"]]
                    steps.append(s_red)
            return steps

        bg_a0 = unit_a(0)
        for u in [unit_rows("p", 0), unit_rows("p", 1)] + [unit_rows("c", i) for i in range(4)] + [unit_a(1)]:
            bg.extend(u)

        for ia in range(2):
            out_toks.append(sp_dma(nas[ia, :, 0, :], sa[ia, :, 1, :], "dd"))
        out_toks.append(sp_dma(nbs[:, 0:14, :], spl[:, 1:15, :], "dd"))
        out_toks.append(sp_dma(ncs[:, 0:29, :], sc[:, 1:30, :], "dd"))

        def rmsnorm(grow, dst, sqbuf, war, extra=None, lookahead=True):
            out_tok = [None] * len(TILES)
            st = {}
            last_pe = [None]

            def s1(ti):
                t0, n = TILES[ti]
                a1 = act(sqbuf[:, 0:4, t0:t0 + n], X[:, 0:4, t0:t0 + n], AF.Square, x_tok[ti] + war)
                a1b = act(sqbuf[:, 4:8, t0:t0 + n], X[:, 4:8, t0:t0 + n], AF.Square, x_tok[ti] + war)
                b, bw = BK.alloc()
                tk = None
                for c in range(NCH):
                    tk = mm(PS[:, b, 0:n], ONES, sqbuf[:, c, t0:t0 + n], c == 0, c == NCH - 1,
                            [a1 if c < 4 else a1b, t_ones] + bw, inc=(c == NCH - 1))
                st[ti] = (b, tk, a1, a1b)
                last_pe[0] = tk

            def s2(ti):
                t0, n = TILES[ti]
                b, tk, a1, a1b = st[ti]
                a2 = act(PS[:, b, 0:n], PS[:, b, 0:n], AF.Ln, [tk, t_eps], bias=EPSt.ap(), scale=1.0)
                a3 = act(PS[:, b, 0:n], PS[:, b, 0:n], AF.Exp, [a2], scale=-0.5)
                toks = []
                dl = None
                for c in range(NCH):
                    dl = dve_stt(dst[:, c, t0:t0 + n], X[:, c, t0:t0 + n], prm(c, grow), PS[:, b, 0:n],
                                 ALU.mult, ALU.mult, [a3, t_param] + x_tok[ti] + war)
                    toks.append(dl)
                if extra is not None:
                    ex = extra(ti, PS[:, b, 0:n], a3)
                    if ex is not None:
                        dl = ex
                BK.release(b, [dl])
                x_rd[ti] = [a1, a1b]
                out_tok[ti] = toks

            nt = len(TILES)
            if lookahead:
                s1(0)
                for ti in range(nt):
                    if ti + 1 < nt:
                        s1(ti + 1)
                    s2(ti)
            else:
                for ti in range(nt):
                    s1(ti)
                    s2(ti)
            return out_tok, last_pe[0]

        def per_k(toklists):
            return lambda ti, k: [toklists[ti][k]]

        def per_tile(toklists):
            return lambda ti, k: toklists[ti]

        def proj(slabs, nk, rhs_fn, rhs_tok, evac, wslice=None, first_to=False, last_to=False, all_to=False):
            last = None
            if all_to:
                assert len(slabs) <= NSLAB
                ld = [RG.load(views) for (views, chunks) in slabs]
                tk = None
                for ti in range(len(TILES)):
                    t0, n = TILES[ti]
                    for si, (views, chunks) in enumerate(slabs):
                        s, slot, rdy = ld[si]
                        for j, ch in enumerate(chunks):
                            b, bw = BK.alloc()
                            for k in range(nk):
                                tk = mm(PS[:, b, 0:n], wslice(slot, j, k), rhs_fn(ch, k, t0, n), k == 0,
                                        k == nk - 1, rdy + bw + rhs_tok(ti, k), inc=(k == nk - 1))
                            BK.release(b, evac(ch, ti, t0, n, PS[:, b, 0:n], tk))
                for (s, slot, rdy) in ld:
                    RG.done(s, tk)
                return tk
            for si, (views, chunks) in enumerate(slabs):
                s, slot, rdy = RG.load(views)
                to = (first_to and si == 0) or (last_to and si == len(slabs) - 1)
                if to:
                    order = [(j, ti) for ti in range(len(TILES)) for j in range(len(chunks))]
                else:
                    order = [(j, ti) for j in range(len(chunks)) for ti in range(len(TILES))]
                tk = None
                for (j, ti) in order:
                    ch = chunks[j]
                    t0, n = TILES[ti]
                    b, bw = BK.alloc()
                    for k in range(nk):
                        lhsT = wslice(slot, j, k)
                        tk = mm(PS[:, b, 0:n], lhsT, rhs_fn(ch, k, t0, n), k == 0, k == nk - 1,
                                rdy + bw + rhs_tok(ti, k), inc=(k == nk - 1))
                    rel = evac(ch, ti, t0, n, PS[:, b, 0:n], tk)
                    BK.release(b, rel)
                RG.done(s, tk)
                last = tk
            return last

        def wstd(slot, j, k):
            return slot.rearrange("p (k n) -> p k n", k=8)[:, k, j * 128:(j + 1) * 128]

        def resid_evac(bias_row=None):
            newtok = [[] for _ in TILES]

            def ev(ch, ti, t0, n, bank, tk):
                w = [tk] + x_rd[ti]
                if bias_row is None:
                    e = dve_tt(X[:, ch, t0:t0 + n], bank, X[:, ch, t0:t0 + n], ALU.add, w)
                else:
                    e = dve_stt(X[:, ch, t0:t0 + n], bank, prm(ch, bias_row), X[:, ch, t0:t0 + n],
                                ALU.add, ALU.add, w + [t_param])
                newtok[ti] = [e]
                return [e]
            return ev, newtok

        def mlp(li, xn_tok, a1_war_tok, hook=None):
            w1 = mlp_w1[li]
            w2 = mlp_w2[li]
            Hb = [H0, H1]
            h_war = [[], []]
            h_tok = [None, None]
            rt_n = [0]
            rt_free = [[] for _ in range(4)]
            rnew = [[[] for _ in TILES]]

            def do_w1(g):
                hb = Hb[g % 2]
                toks = [[] for _ in TILES]

                def ev(ch, ti, t0, n, bank, tk):
                    r = rt_n[0] % 4
                    rt_n[0] += 1
                    rt = SS[:, r, 0:n]
                    a = act(rt, bank, AF.Relu, [tk] + rt_free[r])
                    d = dve_tt(hb[:, ch, t0:t0 + n], rt, bank, ALU.mult, [a] + h_war[g % 2])
                    rt_free[r] = [d]
                    toks[ti] = [d]
                    bg_step(1)
                    return [d]
                slabs = [(slab_std(w1, 0, g * 1024 + s * 256), [2 * s, 2 * s + 1]) for s in range(4)]
                proj(slabs, 8, lambda ch, k, t0, n: XN[:, k, t0:t0 + n], per_k(xn_tok), ev, wslice=wstd,
                     first_to=(g == 0))
                h_tok[g % 2] = toks

            def do_w2(g):
                hb = Hb[g % 2]
                ev, newtok = resid_evac()
                slabs = [(slab_std(w2, g * 1024, s * 256), [2 * s, 2 * s + 1]) for s in range(4)]
                last = proj(slabs, 8, lambda ch, k, t0, n: hb[:, k, t0:t0 + n], per_tile(h_tok[g % 2]), ev,
                            wslice=wstd, all_to=(g == 3))
                h_war[g % 2] = [last]
                for ti in range(len(TILES)):
                    x_tok[ti] = newtok[ti]
                    x_rd[ti] = []

            h_war[0] = list(a1_war_tok)
            h_war[1] = list(a1_war_tok)
            do_w1(0)
            for g in range(1, 4):
                do_w1(g)
                if g == 3:
                    bg_step(10 ** 9)
                if g == 3 and hook is not None:
                    hook([P.cnt["pe"] and ("pe", P.cnt["pe"])])
                do_w2(g - 1)
            do_w2(3)

        def mixer_a(ia, xn_tok, last_stat_pe):
            win = a_w_in[ia]
            wout = a_w_out[ia]
            ch_war = [[last_stat_pe], [last_stat_pe]]
            z_war = [[last_stat_pe], [last_stat_pe]]
            bz_tok = [[] for _ in TILES]
            hs_tok = {}
            z_tok = {}
            pad = dve_memset(CH[:, :, 0:2], 0.0, [last_stat_pe])

            import collections
            pend = collections.deque()
            key_last = {}
            seq = [0]

            def flush_n(n):
                while pend and n > 0:
                    pend.popleft()[1]()
                    n -= 1

            def flush_until(key):
                lim = key_last[key]
                while pend and pend[0][0] <= lim:
                    pend.popleft()[1]()

            def push_conv(a, ti, ch, e):
                w0, w1, w2 = (prm(ch, R_ACONV + 3 * ia + j) for j in range(3))
                st = {}
                zw = list(z_war[a])
                ops = []
                if ti < 4:
                    t0, n = TILES[ti]

                    def f1():
                        st["z"] = dve_ts(ZA[:, a, t0:t0 + n], CH[:, a, t0:t0 + n], w0, None, ALU.mult, None,
                                         [e, pad, t_param] + zw)

                    def f2():
                        st["z"] = dve_stt(ZA[:, a, t0:t0 + n], CH[:, a, t0 + 1:t0 + 1 + n], w1,
                                          ZA[:, a, t0:t0 + n], ALU.mult, ALU.add, [st["z"]])

                    def f3():
                        st["z"] = dve_stt(ZA[:, a, t0:t0 + n], CH[:, a, t0 + 2:t0 + 2 + n], w2,
                                          ZA[:, a, t0:t0 + n], ALU.mult, ALU.add, [st["z"]])
                        z_tok[(a, ti)] = st["z"]
                    ops = [f1, f2, f3]
                    if ti == 3:
                        def f4():
                            st["z"] = dve_copy(TAt.ap()[:, ia, ch, :], CH[:, a, TP:TP + 2], [st["z"]])
                            tail_toks.append(st["z"])
                        ops.append(f4)
                else:
                    def g1():
                        st["z"] = dve_stt(ZA[:, a, TP:T], CH[:, a, 2 + TP:2 + T], w2, HAt.ap()[:, ia, ch, :],
                                          ALU.mult, ALU.add, [e, t_param] + pre["a%d" % ia] + zw)
                        z_tok[(a, ti)] = st["z"]

                    def g2():
                        st["z"] = dve_copy(CHSt.ap()[:, ia, ch, :], CH[:, a, 2 + TP:2 + T], [st["z"]])
                        ch_war[a] = [st["z"]]
                        tail_toks.append(st["z"])
                    ops = [g1, g2]
                for f in ops:
                    seq[0] += 1
                    pend.append((seq[0], f))
                key_last[(a, ti)] = seq[0]

            for q in range(4):
                def ev_h(ch, ti, t0, n, bank, tk):
                    a = ch % 2
                    e = act(CH[:, a, 2 + t0:2 + t0 + n], bank, AF.Copy, [tk] + ch_war[a])
                    hs_tok[(a, ti)] = e
                    return [e]

                def ev_c(ch, ti, t0, n, bank, tk):
                    a = ch % 2
                    e = dve_tt(CH[:, a, 2 + t0:2 + t0 + n], bank, CH[:, a, 2 + t0:2 + t0 + n], ALU.mult,
                               [tk, hs_tok[(a, ti)]])
                    push_conv(a, ti, ch, e)
                    flush_n(2)
                    return [e]

                def ev_b(ch, ti, t0, n, bank, tk):
                    a = ch % 2
                    flush_until((a, ti))
                    e = dve_tt(A2B[:, ch, t0:t0 + n], bank, ZA[:, a, t0:t0 + n], ALU.mult,
                               [tk, z_tok[(a, ti)]])
                    bz_tok[ti] = [e]
                    if ti == 4:
                        z_war[a] = [e]
                    return [e]

                chs = [2 * q, 2 * q + 1]
                rf = (lambda ch, k, t0, n: XN[:, k, t0:t0 + n])
                proj([(slab_std(win, 0, 2048 + 256 * q), chs)], 8, rf, per_k(xn_tok), ev_h, wslice=wstd,
                     first_to=(q == 0))
                proj([(slab_std(win, 0, 1024 + 256 * q), chs)], 8, rf, per_k(xn_tok), ev_c, wslice=wstd)
                proj([(slab_std(win, 0, 256 * q), chs)], 8, rf, per_k(xn_tok), ev_b, wslice=wstd)
                assert not pend

            war = snap()
            ev, newtok = resid_evac()
            slabs = [(slab_std(wout, 0, s * 256), [2 * s, 2 * s + 1]) for s in range(4)]
            last = proj(slabs, 8, lambda ch, k, t0, n: A2B[:, k, t0:t0 + n], per_tile(bz_tok), ev, wslice=wstd,
                        all_to=True)
            for ti in range(len(TILES)):
                x_tok[ti] = newtok[ti]
                x_rd[ti] = []
            return war

        def mixer_b(xn_tok, last_stat_pe, grow):
            Dm = A2B
            coefs = [1.0 / w for w in POOLW] + [1.0 / w - 1.0 for w in POOLW]
            dgt = None
            for r, cf in enumerate(coefs):
                dgt = act(DG[:, r, :], IDB, AF.Copy, [t_idb, last_stat_pe], scale=float(cf))
            d_tok = [[] for _ in TILES]
            allx = [t for l in xn_tok for t in l]
            for ti in range(4):
                t0 = ti * 512
                xw = list(xn_tok[ti]) + (list(xn_tok[ti - 1]) if ti > 0 else [])
                for c in range(NCH):
                    gi = c // 2
                    w = POOLW[gi]
                    b, bw = BK.alloc()
                    tk = None
                    for j in range(w):
                        lhsT = DG[:, (4 + gi) if j == 0 else gi, :]
                        if ti == 0:
                            o = PS[:, b, j:512]
                            r_ = XN[:, c, 0:512 - j]
                        else:
                            o = PS[:, b, :]
                            r_ = XN[:, c, t0 - j:t0 - j + 512]
                        tk = mm(o, lhsT, r_, j == 0, j == w - 1, [dgt] + bw + xw, inc=(j == w - 1))
                    e = act(Dm[:, c, t0:t0 + 512], PS[:, b, :], AF.Copy, [tk, last_stat_pe])
                    rel = [e]
                    if ti == 0 and w > 1:
                        hw = w - 1
                        f1 = dve_tt(CTMP[:, 0:hw], PS[:, b, 0:hw], XN[:, c, 0:hw], ALU.add, [tk, e] + ctl())
                        f2 = dve_tt(CTMP[:, 0:hw], CTMP[:, 0:hw], RWt.ap()[:, gi, 0:hw], ALU.mult, [f1, t_rw])
                        f3 = dve_tt(Dm[:, c, 0:hw], CTMP[:, 0:hw], XN[:, c, 0:hw], ALU.subtract, [f2, e])
                        ctl_[0] = f3
                        rel = [e, f1]
                        e = f3
                    BK.release(b, rel)
                    d_tok[ti] = [e]
            for c in range(NCH):
                w = POOLW[c // 2]
                e5 = dve_tt(CTMP[:, 64:64 + TS], HPt.ap()[:, c, :], XNSt.ap()[:, c, :], ALU.add,
                            allx + pre["p"] + [last_stat_pe] + ctl())
                e6 = dve_stt(Dm[:, c, TP:T], CTMP[:, 64:64 + TS], 1.0 / w, XNSt.ap()[:, c, :],
                             ALU.mult, ALU.subtract, [e5])
                ctl_[0] = e6
                d_tok[4] = [e6]
            alld = [t for l in d_tok for t in l]
            war = snap()
            newtok = [[] for _ in TILES]

            def evb(ch, ti, t0, n, bank, tk):
                e = dve_stt(X[:, ch, t0:t0 + n], bank, prm(ch, R_BSCALE), X[:, ch, t0:t0 + n],
                            ALU.mult, ALU.add, [tk, t_param] + x_rd[ti])
                newtok[ti] = [e]
                return [e]
            src = b_w_group.rearrange("g (k p) n -> p g k n", p=128)
            views = [((lambda slot: slot.rearrange("p (g k n) -> p g k n", g=4, k=2)), src)]
            s, slot, rdy = RG.load(views)
            sv = slot.rearrange("p (g k n) -> p g k n", g=4, k=2)
            tk = None
            for ti, (t0, n) in enumerate(TILES):
                for ch in range(NCH):
                    gi, a = ch // 2, ch % 2
                    b, bw = BK.alloc()
                    for k in range(2):
                        tk = mm(PS[:, b, 0:n], sv[:, gi, k, a * 128:(a + 1) * 128], Dm[:, 2 * gi + k, t0:t0 + n],
                                k == 0, k == 1, rdy + bw + alld, inc=(k == 1))
                    BK.release(b, evb(ch, ti, t0, n, PS[:, b, 0:n], tk))
            RG.done(s, tk)
            for ti in range(len(TILES)):
                x_tok[ti] = newtok[ti]
                x_rd[ti] = []
            return war

        def mixer_c(xn_tok, last_stat_pe):
            sg_tok = {}
            u_tok = [[] for _ in TILES]
            sig_war = [[last_stat_pe], [last_stat_pe]]
            padu = dve_memset(U[:, :, 0:30], 0.0, [last_stat_pe])
            lastu = [None]
            for q in range(4):
                def ev_g(ch, ti, t0, n, bank, tk):
                    a = ch % 2
                    e = act(SIG[:, a, t0:t0 + n], bank, AF.Sigmoid, [tk, t_param] + sig_war[a],
                            bias=prm(ch, R_BPW1G), scale=1.0)
                    sg_tok[(a, ti)] = e
                    return [e]

                def ev_a(ch, ti, t0, n, bank, tk):
                    a = ch % 2
                    w = [tk, sg_tok[(a, ti)], t_param, padu]
                    if ti < 4:
                        e = dve_stt(U[:, ch, 30 + t0:30 + t0 + n], bank, prm(ch, R_BPW1A), SIG[:, a, t0:t0 + n],
                                    ALU.add, ALU.mult, w)
                        if ti == 3:
                            e = dve_stt(TCt.ap()[:, ch, :], bank[:, 512 - 30:512], prm(ch, R_BPW1A),
                                        SIG[:, a, TP - 30:TP], ALU.add, ALU.mult, [e])
                    else:
                        e = dve_stt(USt.ap()[:, ch, :], bank, prm(ch, R_BPW1A), SIG[:, a, TP:T],
                                    ALU.add, ALU.mult, w)
                        sig_war[a] = [e]
                    u_tok[ti] = [e]
                    lastu[0] = e
                    return [e]
                chs = [2 * q, 2 * q + 1]
                rf = (lambda ch, k, t0, n: XN[:, k, t0:t0 + n])
                proj([(slab_std(c_w_pw1, 0, 1024 + 256 * q), chs)], 8, rf, per_k(xn_tok), ev_g, wslice=wstd,
                     first_to=(q == 0))
                last_pw1 = proj([(slab_std(c_w_pw1, 0, 256 * q), chs)], 8, rf, per_k(xn_tok), ev_a, wslice=wstd)
            dg_n = 0
            dg_free = [[last_stat_pe] for _ in range(8)]
            z_tok = [[] for _ in TILES]
            zlast = None
            for c in range(NCH):
                bks = [BK.alloc() for _ in range(4)]
                tk = None
                for j in range(31):
                    r = dg_n % 8
                    dg_n += 1
                    dtk = act(DG[:, r, :], IDB, AF.Copy, [t_idb, t_param] + dg_free[r],
                              scale=prm(c, R_CCONV + j))
                    for ti in range(4):
                        t0 = ti * 512
                        b, bw = bks[ti]
                        tk = mm(PS[:, b, :], DG[:, r, :], U[:, c, t0 + j:t0 + j + 512], j == 0, j == 30,
                                [dtk, lastu[0], last_pw1] + (bw if j == 0 else []), inc=(j == 30 or ti == 3))
                    dg_free[r] = [tk]
                for ti in range(4):
                    t0 = ti * 512
                    b, _ = bks[ti]
                    e = dve_ts(ZC[:, c, t0:t0 + 512], PS[:, b, :], prm(c, R_CONVB), None, ALU.add, None,
                               [tk, last_pw1, lastu[0], t_param])
                    BK.release(b, [e])
                    zlast = e
                e1 = dve_stt(ZC[:, c, TP:T], USt.ap()[:, c, :], prm(c, R_CCONV + 30), HCt.ap()[:, c, :],
                             ALU.mult, ALU.add, [last_pw1, lastu[0], t_param] + pre["c"])
                e2 = dve_ts(ZC[:, c, TP:T], ZC[:, c, TP:T], prm(c, R_CONVB), None, ALU.add, None, [e1])
                zs_last = e2
            conv_pe = tk
            SL = A2B
            ZBQ = SSt.ap().rearrange("p a t -> p (a t)").bitcast(BF16).rearrange(
                "p (b q c t) -> p b q c t", b=2, q=2, c=NCH)
            HT = [(t0, 256) for t0 in range(0, TP, 256)] + [(TP, TS)]
            sl_tok = [[] for _ in TILES]
            zq_free = [[], []]
            hb = {}

            def stage_a(h):
                t0, n = HT[h]
                bf = h % 2
                a1 = act(ZBQ[:, bf, 0, :, 0:n], ZC[:, :, t0:t0 + n], AF.Copy, [zlast, zs_last] + zq_free[bf])
                a2 = act(ZBQ[:, bf, 1, :, 0:n], ZC[:, :, t0:t0 + n], AF.Square, [a1])
                b, bw = BK.alloc()
                tk = None
                for c in range(NCH):
                    tk = mm(PS[:, b, 0:n], ONES, ZBQ[:, bf, 0, c, 0:n], c == 0, c == NCH - 1, [a1, a2] + bw,
                            inc=False)
                for c in range(NCH):
                    tk = mm(PS[:, b, 256:256 + n], ONES, ZBQ[:, bf, 1, c, 0:n], c == 0, c == NCH - 1, [a2],
                            inc=(c == NCH - 1))
                zq_free[bf] = [tk]
                hb[h] = (b, tk)

            def stage_le(h):
                t0, n = HT[h]
                b, tk = hb[h]
                mu = PS[:, b, 0:n]
                vr = PS[:, b, 256:256 + n]
                ms = MUS[:, (h % 2) * 256:(h % 2) * 256 + n]
                c0 = dve_copy(ms, mu, [tk])
                c1 = dve_tt(ms, ms, mu, ALU.mult, [c0])
                c2 = dve_tt(vr, vr, ms, ALU.subtract, [c1])
                a3 = act(vr, vr, AF.Ln, [c2, t_eps], bias=EPSt.ap(), scale=1.0)
                a4 = act(vr, vr, AF.Exp, [a3], scale=-0.5)
                hb[h] = (b, a4)

            def stage_ap(h):
                t0, n = HT[h]
                b, a4 = hb[h]
                mu = PS[:, b, 0:n]
                vr = PS[:, b, 256:256 + n]
                dl = None
                d3 = None
                for c in range(NCH):
                    d2 = dve_tt(ZC[:, c, t0:t0 + n], ZC[:, c, t0:t0 + n], mu, ALU.subtract, [a4])
                    d3 = dve_tt(ZC[:, c, t0:t0 + n], ZC[:, c, t0:t0 + n], vr, ALU.mult, [d2])
                    dl = act(SL[:, c, t0:t0 + n], ZC[:, c, t0:t0 + n], AF.Silu, [d3, conv_pe, t_param],
                             bias=prm(c, R_LNB), scale=prm(c, R_LNG))
                BK.release(b, [d3])
                sl_tok[min(t0 // 512, 4)] = [dl]

            pairs = [(0, 1), (2, 3), (4, 5), (6, 7), (8,)]
            for h in pairs[0]:
                stage_a(h)
            for pi, pr in enumerate(pairs):
                for h in pr:
                    stage_le(h)
                if pi + 1 < len(pairs):
                    for h in pairs[pi + 1]:
                        stage_a(h)
                for h in pr:
                    stage_ap(h)
            war = snap()
            ev, newtok = resid_evac(R_BPW2)
            slabs = [(slab_std(c_w_pw2, 0, s * 256), [2 * s, 2 * s + 1]) for s in range(4)]
            last = proj(slabs, 8, lambda ch, k, t0, n: SL[:, k, t0:t0 + n], per_tile(sl_tok), ev, wslice=wstd,
                        all_to=True)
            for ti in range(len(TILES)):
                x_tok[ti] = newtok[ti]
                x_rd[ti] = []
            return war

        RCNt = es.enter_context(nc.sbuf_tensor("RCN", [128, 16], F32))
        RCN = RCNt.ap()
        t_rcn = dve_recip(RCN, CNTt.ap(), [t_cnt])
        RWt = es.enter_context(nc.sbuf_tensor("RW", [128, 4, 16], F32))
        t_rw = None
        for gi_, w_ in enumerate(POOLW):
            t_rw = dve_ts(RWt.ap()[:, gi_, :], RCN, float(w_), None, ALU.mult, None, [t_rcn])
        MUSt = es.enter_context(nc.sbuf_tensor("MUS", [128, 512], F32))
        MUS = MUSt.ap()

        NOST = 4
        ost_free = [[] for _ in range(NOST)]
        ost_n = [0]
        ost_flip = [0]
        tail_toks = []
        small_done = [False]

        def emit_rows(src_fn, nrow, dst_ap, src_tok):
            bf = ost_n[0] % NOST
            ost_n[0] += 1
            evs = []
            for half in range(2):
                b, bw = BK.alloc()
                tk = None
                for cc in range(4):
                    c = half * 4 + cc
                    tk = tp(PS[0:nrow, b, cc * 128:(cc + 1) * 128], src_fn(c), IDENT, [t_id] + bw + src_tok,
                            inc=(cc == 3))
                o = OST[0:nrow, bf, half * 512:(half + 1) * 512]
                if ost_flip[0] == 0:
                    e = act(o, PS[0:nrow, b, :], AF.Copy, [tk] + ost_free[bf])
                else:
                    e = dve_copy(o, PS[0:nrow, b, :], [tk] + ost_free[bf])
                ost_flip[0] ^= 1
                BK.release(b, [e])
                evs.append(e)
            d = sp_dma(dst_ap, OST[0:nrow, bf, :], "o%d" % bf, waits=evs)
            ost_free[bf] = [d]
            out_toks.append(d)

        def emit_small(src_tok):
            small_done[0] = True
            na_done = min(2, (depth_run + 2) // 3)
            for ia_ in range(na_done):
                emit_rows(lambda c, ia_=ia_: TAt.ap()[:, ia_, c, :], 2, nap[ia_], src_tok)
                emit_rows(lambda c, ia_=ia_: CHSt.ap()[:, ia_, c, :], TS, nas[ia_, :, 1, :], src_tok)
            if depth_run >= 2:
                emit_rows(lambda c: TBt.ap()[:, c, :], 15, nbp[:, :], src_tok)
                emit_rows(lambda c: XNSt.ap()[:, c, :], TS, nbs[:, 14, :], src_tok)
            if depth_run >= 3:
                emit_rows(lambda c: TCt.ap()[:, c, :], 30, ncp[:, :], src_tok)
                emit_rows(lambda c: USt.ap()[:, c, :], TS, ncs[:, 29, :], src_tok)

        def small_hook(pe_tok):
            for bf in range(NOST):
                ost_free[bf] = list(pe_tok)
            emit_small(list(pe_tok) + tail_toks + [("dve", P.cnt["dve"])])

        a_war = []
        for tt in range(5):
            a_war += x_tok[tt]
        prev_last = a_war
        ia = 0
        for li in range(depth_run):
            kind = li % 3
            if kind == 1:
                def extra_b(ti, bank, rdy, li=li):
                    e = None
                    if ti == 3:
                        for c in range(NCH):
                            e = dve_stt(TBt.ap()[:, c, :], X[:, c, TP - 15:TP], prm(c, R_GMIX + li),
                                        bank[:, 512 - 15:512], ALU.mult, ALU.mult, [rdy, t_param])
                    elif ti == 4:
                        for c in range(NCH):
                            e = dve_stt(XNSt.ap()[:, c, :], X[:, c, TP:T], prm(c, R_GMIX + li),
                                        bank[:, 0:TS], ALU.mult, ALU.mult, [rdy, t_param])
                    return e
                xn_tok, stat_pe = rmsnorm(R_GMIX + li, XN, SQ3, prev_last, extra=extra_b)
                lastm = mixer_b(xn_tok, stat_pe, R_GMIX + li)
            else:
                xn_tok, stat_pe = rmsnorm(R_GMIX + li, XN, SQ3, prev_last)
                if li == 0:
                    for f in bg_a0:
                        f()
                if kind == 0:
                    lastm = mixer_a(ia, xn_tok, stat_pe)
                    ia += 1
                else:
                    lastm = mixer_c(xn_tok, stat_pe)
            xn_tok, stat_pe = rmsnorm(R_GMLP + li, XN, SQ3, lastm)
            mlp(li, xn_tok, [stat_pe], hook=(small_hook if li == depth_run - 1 else None))
            prev_last = []

        YF = X
        y_tok, stat_pe = rmsnorm(R_GFIN, YF, SQ1, prev_last + [t for l in ost_free for t in l])
        for bf in range(NOST):
            ost_free[bf] = ost_free[bf] + [stat_pe]
        for i in range(16):
            emit_rows(lambda c, i=i: YF[:, c, i * 128:(i + 1) * 128], 128, yp[i * 128:(i + 1) * 128, :],
                      y_tok[i // 4])
        emit_rows(lambda c: YF[:, c, TP:T], TS, ys[:, :], y_tok[4])
        if not small_done[0]:
            emit_small([t for l in y_tok for t in l])

        P.wait_only("sp", out_toks)

        with nc.Block() as block:
            def runner(name):
                def f(e):
                    for (fn, wl, inc, dsem) in P.q[name]:
                        for (k, v) in wl:
                            e.wait_ge(sems[k], v)
                        if fn is None:
                            continue
                        ins = fn(e)
                        if inc:
                            ins.then_inc(sems[name], 1)
                        elif dsem is not None:
                            ins.then_inc(sems[dsem], 16)
                return f
            block.tensor(runner("pe"))
            block.scalar(runner("act"))
            block.vector(runner("dve"))
            block.gpsimd(runner("pool"))
            block.sync(runner("sp"))
    return nc


_IN_KEYS = ["norm_mix_g", "norm_mlp_g", "a_w_in", "a_w_out", "mlp_w1", "mlp_w2"]


def make_in_maps(inputs, cores):
    f = lambda a: np.ascontiguousarray(np.asarray(a, dtype=np.float32))
    shared = {
        "norm_mix_g": f(inputs["norm_mix_g"]),
        "norm_mlp_g": f(inputs["norm_mlp_g"]),
        "norm_final_g": f(inputs["norm_final_g"]).reshape(1, D),
        "a_w_in": f(inputs["a_w_in"]),
        "a_conv_w": f(inputs["a_conv_w"]).reshape(6, D),
        "a_w_out": f(inputs["a_w_out"]),
        "b_w_group": f(inputs["b_w_group"]).reshape(4, 256, 256),
        "b_scale": f(inputs["b_scale"]).reshape(1, D),
        "c_w_pw1": f(inputs["c_w_pw1"]).reshape(D, 2 * D),
        "c_b_pw1": f(inputs["c_b_pw1"]).reshape(2, D),
        "c_conv_w": f(inputs["c_conv_w"]).reshape(31, D),
        "c_conv_b": f(inputs["c_conv_b"]).reshape(1, D),
        "c_ln_g": f(inputs["c_ln_g"]).reshape(1, D),
        "c_ln_b": f(inputs["c_ln_b"]).reshape(1, D),
        "c_w_pw2": f(inputs["c_w_pw2"]).reshape(D, D),
        "c_b_pw2": f(inputs["c_b_pw2"]).reshape(1, D),
        "mlp_w1": f(inputs["mlp_w1"]),
        "mlp_w2": f(inputs["mlp_w2"]),
    }
    xp = f(inputs["x_prompt"])
    xs = f(inputs["x_sample"])
    sa = f(inputs["state_conv_a"])
    sp = f(inputs["state_pool"])
    sc = f(inputs["state_conv_c"])
    maps = []
    for c in cores:
        m = dict(shared)
        m["xp"] = np.ascontiguousarray(xp[c])
        m["xs"] = np.ascontiguousarray(xs[16 * c:16 * c + 16, 0, :])
        m["sa"] = np.ascontiguousarray(sa[:, 16 * c:16 * c + 16])
        m["spl"] = np.ascontiguousarray(sp[0, 16 * c:16 * c + 16])
        m["sc"] = np.ascontiguousarray(sc[0, 16 * c:16 * c + 16])
        maps.append(m)
    return maps


def kernel(**inputs):
    n = 8
    nc = build_nc(4)
    maps = make_in_maps(inputs, list(range(n)))
    res = run_bass_kernel_spmd(nc, maps, core_ids=list(range(n)))
    r = res.results
    y_prompt = np.stack([r[c]["yp"] for c in range(n)], axis=0)
    y_sample = np.concatenate([r[c]["ys"] for c in range(n)], axis=0)[:, None, :]
    na_p = np.stack([r[c]["nap"] for c in range(n)], axis=1)
    na_s = np.concatenate([r[c]["nas"] for c in range(n)], axis=1)
    nb_p = np.stack([r[c]["nbp"] for c in range(n)], axis=0)[None]
    nb_s = np.concatenate([r[c]["nbs"] for c in range(n)], axis=0)[None]
    nc_p = np.stack([r[c]["ncp"] for c in range(n)], axis=0)[None]
    nc_s = np.concatenate([r[c]["ncs"] for c in range(n)], axis=0)[None]
    outs = (y_prompt, y_sample, na_p, na_s, nb_p, nb_s, nc_p, nc_s)
    return tuple(np.ascontiguousarray(o, dtype=np.float32) for o in outs)
```
